# Optimizing a Trainium2 kernel written in Bass

```python
import jax, jax.numpy as jnp
from jax import lax
import numpy as np

D_MODEL = 1024
BATCH = 32
SEQ = 256
DEPTH = 1
DEC_BATCH = 8
DEC_SEQ = 1024
PAST_LEN = 256

GRID_W = 64
D_MIX = D_MODEL
HG_HEADS = 4
HG_DK = D_MIX // 8
HG_DV = D_MIX // 8
HG_W = HG_HEADS * HG_DK
RG_W = D_MIX - HG_W
RG_HEADS = 8
RG_BW = RG_W // RG_HEADS
RG_C = 8.0
CONV_W = 4
CONV_PAD = (CONV_W // 2, CONV_W - 1 - CONV_W // 2)
CHUNK = 32
D_IN = 5 * HG_W + 2 * RG_W
SPLITS = [HG_W, 2 * HG_W, 3 * HG_W, 4 * HG_W, 5 * HG_W, 5 * HG_W + RG_W]
PEER_HEADS = 8
PEER_NKEYS = 128
PEER_EXPERTS = PEER_NKEYS * PEER_NKEYS
PEER_TOPK = 16
PEER_DQ = 256
PEER_BLOCK = 128
ALPHA = (2.0 * DEPTH) ** 0.25
BETA = (8.0 * DEPTH) ** -0.25
LN_EPS = 1e-5
RMS_EPS = 1e-6

kernel_name = 'hymba_hgrn2_rglru_peer_diffusion_step'

F32 = jnp.float32


def _layer_norm(x, g, b):
    xf = x.astype(F32)
    mu = jnp.mean(xf, -1, keepdims=True)
    var = jnp.mean(jnp.square(xf - mu), -1, keepdims=True)
    return ((xf - mu) * lax.rsqrt(var + LN_EPS) * g.astype(F32) + b.astype(F32)).astype(x.dtype)


def _dwconv(x, w, b):
    y = lax.conv_general_dilated(x, w[:, None, :].astype(x.dtype), (1,), [CONV_PAD],
                                 dimension_numbers=('NWC', 'WIO', 'NWC'),
                                 feature_group_count=x.shape[-1])
    return y + b.astype(x.dtype)


def _hgrn2_chunk(q, k, v, log_f, s0):
    bsz, t, nh, _ = q.shape
    dv = v.shape[-1]
    nc = t // CHUNK

    def ch(a):
        return a.reshape(bsz, nc, CHUNK, nh, a.shape[-1])
    q, k, v, log_f = ch(q), ch(k), ch(v), ch(log_f)
    b = jnp.cumsum(log_f, axis=2)
    b_last = b[:, :, -1:]
    q_dec = q * jnp.exp(b)
    k_dec = k * jnp.exp(-b)
    k_end = k * jnp.exp(b_last - b)
    mask = jnp.tril(jnp.ones((CHUNK, CHUNK), dtype=bool))
    att = jnp.where(mask, jnp.einsum('bnchd,bnshd->bnhcs', q_dec, k_dec), 0.0)
    o_intra = jnp.einsum('bnhcs,bnshe->bnche', att, v)
    d_state = jnp.einsum('bnchd,bnche->nbhde', k_end, v)
    decay = jnp.moveaxis(jnp.exp(b_last[:, :, 0]), 1, 0)

    def step(s, inp):
        g_n, ds_n = inp
        return g_n[..., None] * s + ds_n, s
    s_final, s_prev = lax.scan(step, s0, (decay, d_state))
    o_inter = jnp.einsum('bnchd,nbhde->bnche', q_dec, s_prev)
    return (o_intra + o_inter).reshape(bsz, t, nh, dv), s_final


def _hgrn2_bidir(q, v, f_pre, g, lb, norm_g, s0):
    bsz, t, _ = q.shape

    def heads(a):
        return a.astype(F32).reshape(bsz, t, HG_HEADS, -1)
    qh = heads(jax.nn.silu(q))
    vh = heads(v)
    outs, states = [], []
    for d in range(2):
        z = f_pre[d].astype(F32)
        f = lb[d] + (1.0 - lb[d]) * jax.nn.sigmoid(z)
        k = (1.0 - lb[d]) * jax.nn.sigmoid(-z)
        args = (qh, heads(k), vh, jnp.log(heads(f)))
        if d == 1:
            args = tuple(jnp.flip(a, axis=1) for a in args)
        od, sd = _hgrn2_chunk(*args, s0[:, d].astype(F32))
        outs.append(od if d == 0 else jnp.flip(od, axis=1))
        states.append(sd)
    o = outs[0] + outs[1]
    o = o * lax.rsqrt(jnp.mean(o * o, -1, keepdims=True) + RMS_EPS) * norm_g.astype(F32).reshape(HG_HEADS, HG_DV)
    o = o.reshape(bsz, t, HG_W).astype(g.dtype) * jax.nn.silu(g)
    return o, jnp.stack(states, axis=1)


def _lin_comb(left, right):
    a1, u1 = left
    a2, u2 = right
    return a1 * a2, a2 * u1 + u2


def _rglru(x, w_r, b_r, w_i, b_i, lam, h0):
    bsz, t, _ = x.shape
    xb = x.reshape(bsz, t, RG_HEADS, RG_BW)

    def gate(w, b):
        return jax.nn.sigmoid(jnp.einsum('btnc,ncd->btnd', xb, w.astype(F32)).reshape(bsz, t, RG_W) + b.astype(F32))
    r = gate(w_r, b_r)
    i = gate(w_i, b_i)
    log_a = -RG_C * r * jax.nn.softplus(-lam.astype(F32))
    a = jnp.exp(log_a)
    u = jnp.sqrt(-jnp.expm1(2.0 * log_a)) * (i * x)
    a_cum, h = lax.associative_scan(_lin_comb, (a, u), axis=1)
    h = h + a_cum * h0[:, None, :]
    return h, h[:, -1]


def _rglru_bidir(x, lp, h0):
    x = x.astype(F32)
    hs, states = [], []
    for d in range(2):
        xd = x if d == 0 else jnp.flip(x, axis=1)
        h, hl = _rglru(xd, lp['rg_wr'][d], lp['rg_br'][d], lp['rg_wi'][d], lp['rg_bi'][d],
                       lp['rg_lam'][d], h0[:, d].astype(F32))
        hs.append(h if d == 0 else jnp.flip(h, axis=1))
        states.append(hl)
    return hs[0] + hs[1], jnp.stack(states, axis=1)


def _token_mixer(h, s_hg0, s_rg0, grid, lp):
    bsz, t, _ = h.shape
    z = h @ lp['w_in']
    q, iv, f_f, f_b, g, xr, gr = jnp.split(z, SPLITS, axis=-1)
    o_hg, s_hg = _hgrn2_bidir(q, iv, (f_f, f_b), g, lp['lb'], lp['hg_norm'], s_hg0)
    if grid:
        rows = t // GRID_W
        xr = xr.reshape(bsz, rows, GRID_W, RG_W).transpose(0, 2, 1, 3)
        xc = _dwconv(xr.reshape(bsz * GRID_W, rows, RG_W), lp['conv_w'], lp['conv_b'])
        xc = xc.reshape(bsz, GRID_W * rows, RG_W)
    else:
        xc = _dwconv(xr, lp['conv_w'], lp['conv_b'])
    hr, s_rg = _rglru_bidir(xc, lp, s_rg0)
    if grid:
        hr = hr.reshape(bsz, GRID_W, rows, RG_W).transpose(0, 2, 1, 3).reshape(bsz, t, RG_W)
    y_rg = hr.astype(h.dtype) * jax.nn.gelu(gr)
    out = jnp.concatenate([o_hg.astype(h.dtype), y_rg], axis=-1) @ lp['w_out']
    return out, s_hg, s_rg


def _peer(h, wq, keys, u_tab, v_tab):
    bsz, t, d = h.shape
    blocks = h.reshape(-1, PEER_BLOCK, d)

    def blk(xb):
        m = xb.shape[0]
        q = (xb @ wq).reshape(m, PEER_HEADS, 2, PEER_DQ // 2)
        s = jnp.einsum('mhpd,hpkd->mhpk', q, keys)
        sv, si = lax.top_k(s, PEER_TOPK)
        cand = (sv[:, :, 0, :, None] + sv[:, :, 1, None, :]).reshape(m, PEER_HEADS, PEER_TOPK * PEER_TOPK)
        cand_idx = (si[:, :, 0, :, None] * PEER_NKEYS + si[:, :, 1, None, :]).reshape(m, PEER_HEADS, PEER_TOPK * PEER_TOPK)
        fv, fi = lax.top_k(cand, PEER_TOPK)
        idx = jnp.take_along_axis(cand_idx, fi, axis=-1)
        gate = jax.nn.softmax(fv.astype(F32), axis=-1).astype(xb.dtype)
        act = jax.nn.gelu(jnp.einsum('md,mhkd->mhk', xb, u_tab[idx]))
        return jnp.einsum('mhk,mhkd->md', gate * act, v_tab[idx])
    return lax.map(blk, blocks).reshape(bsz, t, d)


def _layer(x, cond, s_hg0, s_rg0, grid, lp):
    mod = jax.nn.silu(cond) @ lp['w_ada'] + lp['b_ada']
    sh1, sc1, g1, sh2, sc2, g2 = jnp.split(mod[:, None, :], 6, axis=-1)
    h = x * (1.0 + sc1) + sh1
    mix, s_hg, s_rg = _token_mixer(h, s_hg0, s_rg0, grid, lp)
    x = _layer_norm(ALPHA * x + g1 * mix, lp['ln1_g'], lp['ln1_b'])
    h = x * (1.0 + sc2) + sh2
    ff = _peer(h, lp['peer_wq'], lp['peer_keys'], lp['peer_u'], lp['peer_v'])
    x = _layer_norm(ALPHA * x + g2 * ff, lp['ln2_g'], lp['ln2_b'])
    return x, s_hg, s_rg


def setup_inputs(seed: int = 0) -> dict:
    key = jax.random.key(seed)
    ks = iter(jax.random.split(key, 40))

    def nrm(shape, scale):
        return jax.random.normal(next(ks), shape, F32) * scale
    u = jax.random.uniform(next(ks), (DEPTH, 2, RG_W), F32, minval=0.9, maxval=0.999)
    s = u ** (1.0 / RG_C)
    rg_lam = jnp.log(s) - jnp.log1p(-s)
    return {
        'x_prompt': nrm((BATCH, SEQ, D_MODEL), 1.0),
        'x_sample': nrm((DEC_BATCH, DEC_SEQ, D_MODEL), 1.0),
        'c': nrm((DEC_BATCH, D_MODEL), 1.0),
        'state_hgrn': nrm((DEC_BATCH, DEPTH, 2, HG_HEADS, HG_DK, HG_DV), 0.3),
        'state_rglru': nrm((DEC_BATCH, DEPTH, 2, RG_W), 0.5),
        'c_ctx': nrm((D_MODEL,), 1.0),
        'w_ada': nrm((DEPTH, D_MODEL, 6 * D_MODEL), 0.3 * D_MODEL ** -0.5),
        'b_ada': nrm((DEPTH, 6 * D_MODEL), 0.02),
        'w_in': nrm((DEPTH, D_MODEL, D_IN), D_MODEL ** -0.5),
        'hgrn_lb': nrm((2, DEPTH + 1, HG_W), 0.1),
        'hgrn_norm_g': 1.0 + nrm((DEPTH, HG_W), 0.02),
        'conv_w': nrm((DEPTH, CONV_W, RG_W), CONV_W ** -0.5),
        'conv_b': nrm((DEPTH, RG_W), 0.02),
        'rg_wr': nrm((DEPTH, 2, RG_HEADS, RG_BW, RG_BW), RG_BW ** -0.5),
        'rg_br': nrm((DEPTH, 2, RG_W), 0.02),
        'rg_wi': nrm((DEPTH, 2, RG_HEADS, RG_BW, RG_BW), RG_BW ** -0.5),
        'rg_bi': nrm((DEPTH, 2, RG_W), 0.02),
        'rg_lam': rg_lam,
        'w_out': nrm((DEPTH, D_MIX, D_MODEL), BETA * D_MIX ** -0.5),
        'ln1_g': 1.0 + nrm((DEPTH, D_MODEL), 0.02),
        'ln1_b': nrm((DEPTH, D_MODEL), 0.02),
        'peer_wq': nrm((DEPTH, D_MODEL, PEER_HEADS * PEER_DQ), D_MODEL ** -0.5),
        'peer_keys': nrm((DEPTH, PEER_HEADS, 2, PEER_NKEYS, PEER_DQ // 2), (PEER_DQ // 2) ** -0.5),
        'peer_u': nrm((DEPTH, PEER_EXPERTS, D_MODEL), D_MODEL ** -0.5),
        'peer_v': nrm((DEPTH, PEER_EXPERTS, D_MODEL), BETA),
        'ln2_g': 1.0 + nrm((DEPTH, D_MODEL), 0.02),
        'ln2_b': nrm((DEPTH, D_MODEL), 0.02),
    }


def reference(x_prompt, x_sample, c, state_hgrn, state_rglru, c_ctx, w_ada, b_ada, w_in,
              hgrn_lb, hgrn_norm_g, conv_w, conv_b, rg_wr, rg_br, rg_wi, rg_bi, rg_lam,
              w_out, ln1_g, ln1_b, peer_wq, peer_keys, peer_u, peer_v, ln2_g, ln2_b):
    lb_all = jnp.cumsum(jax.nn.softmax(hgrn_lb.astype(F32), axis=1), axis=1)
    xp, xs = x_prompt, x_sample
    bp = x_prompt.shape[0]
    new_hg, new_rg = [], []
    for l in range(DEPTH):
        lp = {
            'w_ada': w_ada[l], 'b_ada': b_ada[l], 'w_in': w_in[l], 'lb': lb_all[:, l],
            'hg_norm': hgrn_norm_g[l], 'conv_w': conv_w[l], 'conv_b': conv_b[l],
            'rg_wr': rg_wr[l], 'rg_br': rg_br[l], 'rg_wi': rg_wi[l], 'rg_bi': rg_bi[l],
            'rg_lam': rg_lam[l], 'w_out': w_out[l], 'ln1_g': ln1_g[l], 'ln1_b': ln1_b[l],
            'peer_wq': peer_wq[l], 'peer_keys': peer_keys[l], 'peer_u': peer_u[l],
            'peer_v': peer_v[l], 'ln2_g': ln2_g[l], 'ln2_b': ln2_b[l],
        }
        zero_hg = jnp.zeros((bp, 2, HG_HEADS, HG_DK, HG_DV), F32)
        zero_rg = jnp.zeros((bp, 2, RG_W), F32)
        xp, s_hg, s_rg = _layer(xp, c_ctx[None, :], zero_hg, zero_rg, False, lp)
        new_hg.append(s_hg)
        new_rg.append(s_rg)
        xs, _, _ = _layer(xs, c, state_hgrn[:, l], state_rglru[:, l], True, lp)
    new_state_hgrn = jnp.stack(new_hg, axis=1).astype(x_prompt.dtype)
    new_state_rglru = jnp.stack(new_rg, axis=1).astype(x_prompt.dtype)
    return (xp, xs, new_state_hgrn, new_state_rglru)
```

```python
from contextlib import ExitStack

import numpy as np
import concourse.bass as bass
import concourse.mybir as mybir
from concourse.bass_utils import run_bass_kernel_spmd

F32 = mybir.dt.float32
BF16 = mybir.dt.bfloat16
I32 = mybir.dt.int32
U32 = mybir.dt.uint32
AF = mybir.ActivationFunctionType
ALU = mybir.AluOpType
AX = mybir.AxisListType

D = 1024
NCORES = 8
ALPHA = 2.0 ** 0.25
LN_EPS = 1e-5
RMS_EPS = 1e-6

PP_COND = 0
PP_BADA = 16
PP_LB = 64
PP_CW = 80
PP_CB = 96
PP_BR = 100
PP_BI = 108
PP_LAM = 116
PP_SRG = 124
NPP = 132
BV_NG, BV_L1G, BV_L1B, BV_L2G, BV_L2B, BVN = 0, 512, 1536, 2560, 3584, 4608


class Prog:
    ENG = ('pe', 'act', 'dve', 'pool', 'sp')

    def __init__(self, nc, es, n_dma_sems=28):
        self.nc = nc
        self.q = {e: [] for e in self.ENG}
        self.sem = {e: es.enter_context(nc.semaphore('s_' + e)) for e in self.ENG}
        self.cnt = {e: 0 for e in self.ENG}
        self.pending = {e: False for e in self.ENG}
        self.waited = {e: {} for e in self.ENG}
        self.last_w = {}
        self.readers = {}
        self.dsem = [es.enter_context(nc.semaphore('d%d' % i)) for i in range(n_dma_sems)]
        self.dval = [0] * n_dma_sems
        self.drr = 0
        self.nops = 0
        self.swsem = {}
        self.swsems = []
        self.swgen = []
        self._es = es

    def _semh(self, key):
        if isinstance(key, str):
            return self.sem[key]
        if key[0] == 'sw':
            return self.swsems[key[1]]
        return self.dsem[key[1]]

    def swdma(self, e, meth, slot, R=(), W=(), **kw):
        if slot not in self.swsem:
            self.swsem[slot] = len(self.swsems)
            self.swsems.append(self._es.enter_context(self.nc.semaphore('w%d' % len(self.swsems))))
            self.swgen.append(0)
        i = self.swsem[slot]
        deps = self._deps(R, W)
        if self.swgen[i] > 0:
            deps.append(((('sw', i), self.swgen[i]), True))
        self._emit_waits(e, deps)
        self.swgen[i] += 16
        key = (('sw', i), self.swgen[i])
        self.q[e].append(('swdma', meth, kw, i))
        self._record(key, R, W)
        self.nops += 1

    def _deps(self, R, W):
        deps = []
        for t in R:
            if t in self.last_w:
                deps.append((self.last_w[t], True))
        for t in W:
            if t in self.last_w:
                deps.append((self.last_w[t], False))
            for k in self.readers.get(t, ()):
                deps.append((k, False))
        return deps

    def _emit_waits(self, e, deps):
        need = {}
        for ((k, v), raw) in deps:
            if k == e and e == 'pe':
                continue
            if self.waited[e].get(k, 0) >= v:
                continue
            if need.get(k, 0) < v:
                need[k] = v
        for k, v in need.items():
            self.waited[e][k] = v
            self.q[e].append(('wait', k, v))

    def _record(self, key, R, W):
        for t in W:
            self.last_w[t] = key
            self.readers[t] = []
        for t in R:
            self.readers.setdefault(t, []).append(key)

    def op(self, e, meth, R=(), W=(), inc=True, **kw):
        W = list(W) + [t for t in R if t[0] == 'B' and t not in W]
        self._emit_waits(e, self._deps(R, W))
        if inc:
            self.cnt[e] += 1
            v = self.cnt[e]
            self.pending[e] = False
            self.q[e].append(('op', meth, kw, e))
        else:
            v = self.cnt[e] + 1
            self.pending[e] = True
            self.q[e].append(('op', meth, kw, None))
        self._record((e, v), R, W)
        self.nops += 1

    def dma(self, e, meth, R=(), W=(), **kw):
        i = self.drr % len(self.dsem)
        self.drr += 1
        deps = self._deps(R, W)
        if self.dval[i] > 0:
            deps.append(((('dma', i), self.dval[i]), True))
        self._emit_waits(e, deps)
        self.dval[i] += 16
        key = (('dma', i), self.dval[i])
        self.q[e].append(('dma', meth, kw, i))
        self._record(key, R, W)
        self.nops += 1

    def wait_sw(self, e, slots):
        deps = [((('sw', self.swsem[sl]), self.swgen[self.swsem[sl]]), True) for sl in slots if sl in self.swsem]
        self._emit_waits(e, deps)

    def barrier(self):
        for en in ('pe', 'act', 'dve', 'pool'):
            assert not self.pending[en], en
        for e in self.ENG:
            deps = [((e2, self.cnt[e2]), True) for e2 in self.ENG if e2 != e and self.cnt[e2] > 0]
            deps += [((('dma', i), v), True) for i, v in enumerate(self.dval) if v > 0]
            self._emit_waits(e, deps)
        self.last_w = {}
        self.readers = {}

    def finish(self, e='sp'):
        for en in ('pe', 'act', 'dve', 'pool'):
            assert not self.pending[en], en
        deps = [((('dma', i), v), True) for i, v in enumerate(self.dval) if v > 0]
        deps += [((e2, self.cnt[e2]), True) for e2 in self.ENG if e2 != e and self.cnt[e2] > 0]
        self._emit_waits(e, deps)

    def emit(self):
        nc = self.nc
        with nc.Block() as block:
            def run(e, eng):
                for item in self.q[e]:
                    if item[0] == 'wait':
                        eng.wait_ge(self._semh(item[1]), item[2])
                    elif item[0] == 'op':
                        ins = getattr(eng, item[1])(**item[2])
                        if item[3] is not None:
                            ins.then_inc(self.sem[item[3]], 1)
                    elif item[0] == 'clear':
                        eng.sem_clear(self.swsems[item[1]])
                    elif item[0] == 'swdma':
                        ins = getattr(eng, item[1])(**item[2])
                        ins.then_inc(self.swsems[item[3]], 16)
                    else:
                        ins = getattr(eng, item[1])(**item[2])
                        ins.then_inc(self.dsem[item[3]], 16)

            @block.tensor
            def _(eng):
                run('pe', eng)

            @block.scalar
            def _(eng):
                run('act', eng)

            @block.vector
            def _(eng):
                run('dve', eng)

            @block.gpsimd
            def _(eng):
                run('pool', eng)

            @block.sync
            def _(eng):
                run('sp', eng)


def ln_block(A, src, dst_ap, stats, mv, bvt, og, ob, src_tok, dst_tok):
    for hf in range(2):
        A('dve', 'bn_stats', R=[src_tok], W=['ln_stats'], out=stats[:, hf, :], in_=src[:, hf * 512:(hf + 1) * 512])
    A('dve', 'bn_aggr', R=['ln_stats'], W=['ln_mv'], out=mv[:, 0:2], in_=stats[:].rearrange("p a b -> p (a b)"))
    A('act', 'activation', R=['ln_mv'], W=['ln_mv'], out=mv[:, 2:3], in_=mv[:, 1:2], func=AF.Sqrt, scale=1.0,
      bias=LN_EPS)
    A('dve', 'reciprocal', R=['ln_mv'], W=['ln_mv'], out=mv[:, 2:3], in_=mv[:, 2:3])
    A('dve', 'tensor_scalar', R=['ln_mv'], W=['ln_mv'], out=mv[:, 3:4], in0=mv[:, 0:1], scalar1=mv[:, 2:3],
      scalar2=-1.0, op0=ALU.mult, op1=ALU.mult)
    A('act', 'activation', R=['ln_mv', src_tok], W=[src_tok], out=src[:], in_=src[:], func=AF.Identity,
      scale=mv[:, 2:3], bias=mv[:, 3:4])
    A('dve', 'tensor_tensor', R=[src_tok, 'bvt'], W=[src_tok], out=src[:], in0=src[:], in1=bvt[:, og:og + 1024],
      op=ALU.mult)
    A('dve', 'tensor_tensor', R=[src_tok, 'bvt'], W=[dst_tok], out=dst_ap, in0=src[:], in1=bvt[:, ob:ob + 1024],
      op=ALU.add)


class _StopBuild(Exception):
    pass


def build_program(groups=(0, 1), do_peer=True, debug=False, stop_at=None):
    nc = bass.Bass("TRN2", target_bir_lowering=False)
    try:
        _build(nc, groups, do_peer, debug, stop_at)
    except _StopBuild:
        pass
    return nc


def _build(nc, groups, do_peer, debug, stop_at):

    def din(name, shape, dt=F32):
        return nc.dram_tensor(name, shape, dt, kind="ExternalInput").ap()

    def dout(name, shape, dt=F32):
        return nc.dram_tensor(name, shape, dt, kind="ExternalOutput").ap()

    xg = [din("xp", [1024, D]), din("xs", [1024, D])]
    pp = din("pp", [128, NPP])
    bvd = din("bv", [BVN])
    st_hg = din("st_hg", [2, 4, 128, 128])
    w_ada = din("w_ada", [D, 6 * D])
    w_in = din("w_in", [D, 3584])
    w_out = din("w_out", [D, D])
    wq = din("peer_wq", [D, 2048])
    keys = din("peer_keys", [16, 128, 128])
    rgw = din("rgw", [2, 2, 8, 64, 64])
    pu = din("peer_u", [16384, D])
    pv = din("peer_v", [16384, D])
    uv32 = nc.dram_tensor("uv32", [16384, D], F32, kind="Internal").ap()
    uv16 = uv32.bitcast(BF16)
    yg = [dout("yp", [1024, D]), dout("ys", [1024, D])]
    nhg = dout("nhg", [4, 2, 4, 128, 128])
    nrg = dout("nrg", [128, 32])
    dbg = {}
    if debug:
        dbg['modT'] = dout("d_modT", [128, 96])
        dbg['mixT'] = dout("d_mixT", [2, 128, 8 * 1024], BF16)
        dbg['x1'] = dout("d_x1", [2, 128, 8 * 1024])
        dbg['idx'] = dout("d_idx", [2, 128, 1024], I32)
        dbg['gate'] = dout("d_gate", [2, 128, 1024])

    with ExitStack() as es:
        P = Prog(nc, es)

        def sb(name, shape, dt=F32, scope=es):
            return scope.enter_context(nc.sbuf_tensor(name, shape, dt))

        A = P.op
        DMA = P.dma

        def STOP(name):
            if stop_at == name:
                P.finish()
                P.emit()
                raise _StopBuild()

        Q = [es.enter_context(nc.psum_tensor("Q%d" % i, [128, 1024], F32)) for i in range(4)]
        ident_f = sb("ident_f", [128, 128])
        ident_b = sb("ident_b", [128, 128], BF16)
        ones_f = sb("ones_f", [128, 1024])
        ppt = sb("ppt", [128, NPP])
        bvt = sb("bvt", [128, BVN])
        modT = sb("modT", [128, 48, 2])
        lbt = sb("lbt", [128, 3, 8])
        rgc = sb("rgc", [128, 2, 8])
        keysT = sb("keysT", [128, 16, 128], BF16)
        msk = sb("msk", [128, 2, 128])
        iota16 = sb("iota16", [128, 16])
        rgout = sb("rgout", [128, 32])

        A('pool', 'memset', W=['ident_f'], ap=ident_f[:], constant=1.0)
        A('pool', 'affine_select', R=['ident_f'], W=['ident_f'], out=ident_f[:], in_=ident_f[:], pattern=[[-1, 128]],
          compare_op=ALU.is_equal, fill=0.0, base=0, channel_multiplier=1)
        A('pool', 'tensor_copy', R=['ident_f'], W=['ident_b'], out=ident_b[:], in_=ident_f[:])
        A('pool', 'memset', W=['ones_f'], ap=ones_f[:], constant=1.0)
        A('pool', 'memset', W=['rgout'], ap=rgout[:], constant=0.0)
        A('pool', 'memset', W=['msk'], ap=msk[:], constant=1.0)
        A('pool', 'affine_select', R=['msk'], W=['msk'], out=msk[:, 0, :], in_=msk[:, 0, :], pattern=[[1, 128]],
          compare_op=ALU.is_ge, fill=0.0, base=0, channel_multiplier=-1)
        A('pool', 'affine_select', R=['msk'], W=['msk'], out=msk[:, 1, :], in_=msk[:, 1, :], pattern=[[-1, 128]],
          compare_op=ALU.is_ge, fill=0.0, base=0, channel_multiplier=1)
        for i in range(16):
            A('pool', 'memset', W=['iota16'], ap=iota16[:, i:i + 1], constant=float(i))
        DMA('sp', 'dma_start', W=['ppt'], out=ppt[:], in_=pp[:, :])
        DMA('sp', 'dma_start', W=['bvt'], out=bvt[:], in_=bvd.partition_broadcast(128))

        with ExitStack() as ss:
            wbuf = [sb("wbuf%d" % i, [128, 8, 768], F32, ss) for i in range(2)]
            scond = sb("scond", [128, 8, 2], BF16, ss)
            wb16 = [sb("wb16_%d" % i, [128, 8, 768], BF16, ss) for i in range(2)]
            ks = sb("ks", [128, 16, 128], F32, ss)
            tmp8 = sb("tmp8", [128, 8], F32, ss)
            ps_mod = Q[3][:, 0:96].rearrange("p (j c) -> p j c", c=2)

            A('act', 'activation', R=['ppt'], W=['scond'], out=scond[:].rearrange("p k c -> p (k c)"),
              in_=ppt[:, PP_COND:PP_COND + 16], func=AF.Silu)
            for jb in range(8):
                b = jb % 2
                for k in range(8):
                    DMA('sp', 'dma_start', W=['wbuf%d_%d' % (b, k)], out=wbuf[b][:, k, :],
                        in_=w_ada[k * 128:(k + 1) * 128, jb * 768:(jb + 1) * 768])
                    eng = ('act', 'dve', 'pool', 'dve')[k % 4]
                    A(eng, 'copy' if eng == 'act' else 'tensor_copy', R=['wbuf%d_%d' % (b, k)], W=['wb16_%d_%d' % (b, k)],
                      out=wb16[b][:, k, :], in_=wbuf[b][:, k, :])
                for jj in range(6):
                    j = jb * 6 + jj
                    for k in range(8):
                        A('pe', 'matmul', R=['wb16_%d_%d' % (b, k), 'scond'], W=['B30'], inc=(jj == 5 and k == 7),
                          out=ps_mod[:, j, :], lhsT=wb16[b][:, k, jj * 128:(jj + 1) * 128], rhs=scond[:, k, :],
                          start=(k == 0), stop=(k == 7))
            A('dve', 'tensor_tensor', R=['B30', 'ppt'], W=['modT'], out=modT[:], in0=ps_mod,
              in1=ppt[:, PP_BADA:PP_BADA + 48].unsqueeze(2).to_broadcast([128, 48, 2]), op=ALU.add)
            for base in (8, 32):
                A('dve', 'tensor_scalar_add', R=['modT'], W=['modT'], out=modT[:, base:base + 8, :],
                  in0=modT[:, base:base + 8, :], scalar1=1.0)
            if debug:
                DMA('sp', 'dma_start', R=['modT'], out=dbg['modT'][:, :], in_=modT[:].rearrange("p j c -> p (j c)"))
            lbv = ppt[:, PP_LB:PP_LB + 16].rearrange("p (d s h) -> p d s h", d=2, s=2)
            A('dve', 'tensor_tensor', R=['ppt'], W=['tmp8'], out=tmp8[:].rearrange("p (d h) -> p d h", d=2),
              in0=lbv[:, :, 0, :], in1=lbv[:, :, 1, :], op=ALU.subtract)
            A('act', 'activation', R=['tmp8'], W=['lbt0'], out=lbt[:, 0, :], in_=tmp8[:], func=AF.Sigmoid)
            A('dve', 'tensor_scalar', R=['lbt0'], W=['lbt1'], out=lbt[:, 1, :], in0=lbt[:, 0, :], scalar1=-1.0,
              scalar2=1.0, op0=ALU.mult, op1=ALU.add)
            A('dve', 'tensor_scalar_add', R=['lbt0'], W=['lbt2'], out=lbt[:, 2, :], in0=lbt[:, 0, :], scalar1=-1.0)
            A('act', 'activation', R=['ppt'], W=['rgc0'], out=rgc[:, 0, :], in_=ppt[:, PP_LAM:PP_LAM + 8],
              func=AF.Exp, scale=-1.0)
            A('act', 'activation', R=['rgc0'], W=['rgc0'], out=rgc[:, 0, :], in_=rgc[:, 0, :], func=AF.Ln, scale=1.0,
              bias=1.0)
            A('dve', 'tensor_scalar_mul', R=['rgc0'], W=['rgc1'], out=rgc[:, 1, :], in0=rgc[:, 0, :], scalar1=-16.0)
            A('dve', 'tensor_scalar_mul', R=['rgc0', 'rgc1'], W=['rgc0'], out=rgc[:, 0, :], in0=rgc[:, 0, :],
              scalar1=-8.0)
            DMA('sp', 'dma_start', W=['ks'], out=ks[:], in_=keys.rearrange("a k d -> k a d"))
            for hp in range(16):
                A('pe', 'transpose', R=['ks', 'ident_f'], W=['B%d%d' % (hp // 8, (hp % 8) // 4)],
                  out=Q[hp // 8][:, (hp % 8) * 128:(hp % 8 + 1) * 128], in_=ks[:, hp, :], identity=ident_f[:])
            for i in range(2):
                A('act', 'copy', R=['B%d0' % i, 'B%d1' % i], W=['keysT'],
                  out=keysT[:, i * 8:(i + 1) * 8, :].rearrange("p a k -> p (a k)"), in_=Q[i][:])
            P.barrier()


        for g in groups:
            cnd = g
            with ExitStack() as gs_:
                x1s = sb("x1s%d" % g, [128, 8, D], F32, gs_)
                idx_all = sb("idx%d" % g, [128, 8, 128], I32, gs_)
                gate_all = sb("gate%d" % g, [128, 8, 128], F32, gs_)
                bc = [sb("bc%d_%d" % (g, i), [128, D], F32, gs_) for i in range(4)]
                mx_ = ExitStack()
                mixT = sb("mixT%d" % g, [128, 8, 1024], BF16, mx_)
                Fb = [x1s[:, i, :] for i in range(8)]
                with ExitStack() as ms:
                    xt = [Fb[6], Fb[7]]
                    hT = sb("hT%d" % g, [128, 8, 1024], BF16, ms)
                    wst = [sb("wst%d_%d" % (g, i), [128, 640], F32, ms) for i in range(4)]
                    wib = [sb("wib%d_%d" % (g, i), [128, 8, 640], BF16, ms) for i in range(2)]
                    Hb = [sb("H%d_%d" % (g, i), [128, 1024], BF16, ms) for i in range(6)]
                    vt = sb("vt%d" % g, [128, 8, 128], BF16, ms)
                    gsb = sb("gs%d" % g, [128, 8, 128], BF16, ms)
                    kt = sb("kt%d" % g, [128, 2, 8, 128], BF16, ms)
                    Ssc = sb("Ssc%d" % g, [128, 2, 8, 128], BF16, ms)
                    Sst = [[sb("S%d_%d_%d" % (g, d, i), [128, 128], F32, ms) for i in range(2)] for d in range(2)]
                    stmp = sb("stmp%d" % g, [128, 128], F32, ms)
                    att = [sb("att%d_%d" % (g, i), [128, 2, 128], BF16, ms) for i in range(2)]
                    ogt = [sb("og%d_%d" % (g, i), [128, 128], BF16, ms) for i in range(2)]
                    gtmp = [sb("gtmp%d_%d" % (g, i), [128, 128], F32, ms) for i in range(2)]
                    sqj = sb("sqj%d" % g, [128, 128], F32, ms)
                    bnd = sb("bnd%d" % g, [128, 2, 3, 8], F32, ms)
                    bdf = sb("bdf%d" % g, [128, 2, 3, 8], F32, ms)
                    rcm = sb("rcm%d" % g, [128, 2, 8], F32, ms)
                    epv = sb("epv%d" % g, [128, 2, 8], F32, ms)
                    rms = sb("rms%d" % g, [128, 2, 4], F32, ms)
                    wg_st = sb("wgst%d" % g, [128, 4, 128], F32, ms)
                    wgb = sb("wgb%d" % g, [128, 4, 128], BF16, ms)

                    pX = Q[0]
                    pz = [Q[1][:, 0:512], Q[1][:, 512:1024]]
                    pads = Q[2][:].rearrange("p (a b) -> p a b", b=128)
                    po = Q[3][:, 512:1024].rearrange("p (a b) -> p a b", b=128)
                    pbf = Q[3][:, 0:512].bitcast(BF16).rearrange("p (a b) -> p a b", b=128)

                    for t in range(8):
                        b = t % 2
                        DMA('sp', 'dma_start', W=['xt%d' % b], out=xt[b], in_=xg[g][t * 128:(t + 1) * 128, :])
                        for k in range(8):
                            A('pe', 'transpose', R=['xt%d' % b, 'ident_f'], W=['B0%d' % (k // 4)],
                              out=pX[:, k * 128:(k + 1) * 128], in_=xt[b][:, k * 128:(k + 1) * 128],
                              identity=ident_f[:])
                        import os as _os
                        _dbgv = _os.environ.get('K_DBG', '')
                        for k in range(8):
                            sc = modT[:, 8 + k, cnd:cnd + 1]
                            sh = modT[:, 0 + k, cnd:cnd + 1]
                            if _dbgv == 'noevac':
                                continue
                            if _dbgv == 'plain':
                                A('act', 'copy', R=['B0%d' % (k // 4)], W=['hT%d_%d' % (k, t)],
                                  out=hT[:, k, t * 128:(t + 1) * 128], in_=pX[:, k * 128:(k + 1) * 128])
                                continue
                            if (k < 4 or _dbgv == 'act') and _dbgv != 'dve':
                                A('act', 'activation', R=['B0%d' % (k // 4), 'modT'], W=['hT%d_%d' % (k, t)],
                                  out=hT[:, k, t * 128:(t + 1) * 128], in_=pX[:, k * 128:(k + 1) * 128],
                                  func=AF.Identity, scale=sc, bias=sh)
                            else:
                                A('dve', 'tensor_scalar', R=['B0%d' % (k // 4), 'modT'], W=['hT%d_%d' % (k, t)],
                                  out=hT[:, k, t * 128:(t + 1) * 128], in0=pX[:, k * 128:(k + 1) * 128],
                                  scalar1=sc, scalar2=sh, op0=ALU.mult, op1=ALU.add)
                    P.barrier()
                    STOP('hT')

                    wslot = [0]

                    def load_unit_weights(u, ub):
                        nblk = 5 if u < 4 else 2
                        for k in range(8):
                            s = wslot[0] % 4
                            wslot[0] += 1
                            if u < 4:
                                src = w_in[k * 128:(k + 1) * 128, 0:2560].rearrange(
                                    "p (i h c) -> p i h c", h=4, c=128)[:, :, u, :]
                            else:
                                src = w_in[k * 128:(k + 1) * 128, 2560:3584].rearrange(
                                    "p (i h c) -> p i h c", h=4, c=128)[:, :, u - 4, :]
                            DMA('sp', 'dma_start', W=['wst%d' % s],
                                out=wst[s][:, 0:nblk * 128].rearrange("p (i c) -> p i c", c=128), in_=src)
                            A('pool', 'tensor_copy', R=['wst%d' % s], W=['wib%d_%d' % (ub, k)],
                              out=wib[ub][:, k, 0:nblk * 128], in_=wst[s][:, 0:nblk * 128])

                    zc = [0]

                    def zproj_feat(ub, blk, hf):
                        pi = zc[0] % 2
                        zc[0] += 1
                        for k in range(8):
                            A('pe', 'matmul', R=['wib%d_%d' % (ub, k)], W=['B1%d' % pi], inc=(k == 7),
                              out=pz[pi], lhsT=wib[ub][:, k, blk * 128:(blk + 1) * 128],
                              rhs=hT[:, k, hf * 512:(hf + 1) * 512], start=(k == 0), stop=(k == 7))
                        return pz[pi], 'B1%d' % pi

                    qs, kk, qd = Hb[0], [Hb[1], Hb[2]], [Hb[3], Hb[4]]
                    sg, Eb, Tx = [Fb[0], Fb[1]], [Fb[2], Fb[3]], [Fb[4], Fb[5]]
                    nseq = 4 if g == 0 else 1
                    tps = 8 // nseq
                    load_unit_weights(0, 0)
                    for h in range(4):
                        ub = h % 2
                        for hf in range(2):
                            ps_, tk = zproj_feat(ub, 0, hf)
                            A('act', 'activation', R=[tk], W=['qs'], out=qs[:, hf * 512:(hf + 1) * 512], in_=ps_,
                              func=AF.Silu)
                        for d in range(2):
                            for hf in range(2):
                                ps_, tk = zproj_feat(ub, 2 + d, hf)
                                A('act', 'activation', R=[tk], W=['sg%d' % d], out=sg[d][:, hf * 512:(hf + 1) * 512],
                                  in_=ps_, func=AF.Sigmoid)
                        for t in range(8):
                            pi = zc[0] % 2
                            zc[0] += 1
                            for bi, blk in enumerate((1, 4)):
                                for k in range(8):
                                    A('pe', 'matmul', R=['wib%d_%d' % (ub, k)], W=['B1%d' % pi],
                                      inc=(k == 7 and bi == 1),
                                      out=pz[pi][:, bi * 128:(bi + 1) * 128], lhsT=hT[:, k, t * 128:(t + 1) * 128],
                                      rhs=wib[ub][:, k, blk * 128:(blk + 1) * 128], start=(k == 0), stop=(k == 7))
                            A('act', 'copy', R=['B1%d' % pi], W=['vt%d' % t], out=vt[:, t, :], in_=pz[pi][:, 0:128])
                            gi = t % 2
                            A('act', 'activation', R=['B1%d' % pi], W=['gtmp%d' % gi], out=gtmp[gi][:],
                              in_=pz[pi][:, 128:256], func=AF.Silu)
                            A('dve', 'tensor_tensor', R=['gtmp%d' % gi, 'bvt'], W=['gs%d' % t], out=gsb[:, t, :],
                              in0=gtmp[gi][:], in1=bvt[:, BV_NG + h * 128:BV_NG + (h + 1) * 128], op=ALU.mult)

                        load_unit_weights(h + 1, (h + 1) % 2)
                        STOP('zproj')
                        for d in range(2):
                            lb_c = lbt[:, 0, d * 4 + h:d * 4 + h + 1]
                            oml_c = lbt[:, 1, d * 4 + h:d * 4 + h + 1]
                            noml_c = lbt[:, 2, d * 4 + h:d * 4 + h + 1]
                            A('dve', 'tensor_scalar', R=['sg%d' % d, 'lbt1', 'lbt2'], W=['kk%d' % d], out=kk[d][:],
                              in0=sg[d], scalar1=noml_c, scalar2=oml_c, op0=ALU.mult, op1=ALU.add)
                            A('act', 'activation', R=['sg%d' % d, 'lbt0', 'lbt1'], W=['sg%d' % d], out=sg[d], in_=sg[d],
                              func=AF.Ln, scale=oml_c, bias=lb_c)
                            A('dve', 'tensor_tensor_scan', R=['ones_f', 'sg%d' % d], W=['Eb%d' % d], out=Eb[d],
                              data0=ones_f[:], data1=sg[d], initial=0.0, op0=ALU.mult, op1=ALU.add)
                            V = Eb[d].rearrange("p (t c) -> p t c", c=128)
                            LV = sg[d].rearrange("p (t c) -> p t c", c=128)
                            if d == 0:
                                A('dve', 'memset', W=['epv0'], ap=epv[:, 0, 0:1], constant=0.0)
                                A('dve', 'tensor_copy', R=['Eb0', 'epv0'], W=['epv0'], out=epv[:, 0, 1:8],
                                  in_=V[:, 0:7, 127])
                                A('dve', 'tensor_copy', R=['Eb0'], W=['rcm0'], out=rcm[:, 0, :], in_=V[:, :, 63])
                                A('dve', 'tensor_tensor', R=['rcm0', 'epv0'], W=['bdf0'], out=bdf[:, 0, 0, :],
                                  in0=rcm[:, 0, :], in1=epv[:, 0, :], op=ALU.subtract)
                                A('dve', 'tensor_tensor', R=['Eb0', 'epv0', 'bdf0'], W=['bdf0'], out=bdf[:, 0, 1, :],
                                  in0=V[:, :, 127], in1=epv[:, 0, :], op=ALU.subtract)
                                A('dve', 'tensor_tensor', R=['Eb0', 'rcm0', 'bdf0'], W=['bdf0'], out=bdf[:, 0, 2, :],
                                  in0=V[:, :, 127], in1=rcm[:, 0, :], op=ALU.subtract)
                            else:
                                A('dve', 'tensor_tensor', R=['Eb1', 'sg1'], W=['Eb1'], out=Eb[1], in0=Eb[1], in1=sg[1],
                                  op=ALU.subtract)
                                A('dve', 'tensor_copy', R=['Eb1'], W=['epv1'], out=epv[:, 1, 0:7], in_=V[:, 1:8, 0])
                                A('dve', 'tensor_tensor', R=['Eb1', 'sg1', 'epv1'], W=['epv1'], out=epv[:, 1, 7:8],
                                  in0=V[:, 7, 127:128], in1=LV[:, 7, 127:128], op=ALU.add)
                                A('dve', 'tensor_copy', R=['Eb1'], W=['rcm1'], out=rcm[:, 1, :], in_=V[:, :, 64])
                                A('dve', 'tensor_tensor', R=['rcm1', 'epv1'], W=['bdf1'], out=bdf[:, 1, 0, :],
                                  in0=epv[:, 1, :], in1=rcm[:, 1, :], op=ALU.subtract)
                                A('dve', 'tensor_tensor', R=['Eb1', 'epv1', 'bdf1'], W=['bdf1'], out=bdf[:, 1, 1, :],
                                  in0=epv[:, 1, :], in1=V[:, :, 0], op=ALU.subtract)
                                A('dve', 'tensor_tensor', R=['Eb1', 'rcm1', 'bdf1'], W=['bdf1'], out=bdf[:, 1, 2, :],
                                  in0=rcm[:, 1, :], in1=V[:, :, 0], op=ALU.subtract)
                            A('act', 'activation', R=['bdf%d' % d], W=['bnd%d' % d],
                              out=bnd[:, d, :, :].rearrange("p a t -> p (a t)"),
                              in_=bdf[:, d, :, :].rearrange("p a t -> p (a t)"), func=AF.Exp)
                            A('dve', 'tensor_tensor', R=['Eb%d' % d, 'rcm%d' % d], W=['Eb%d' % d], out=V, in0=V,
                              in1=rcm[:, d, :].unsqueeze(2).to_broadcast([128, 8, 128]), op=ALU.subtract)
                            sq_, sk_ = (1.0, -1.0) if d == 0 else (-1.0, 1.0)
                            A('act', 'activation', R=['Eb%d' % d], W=['Tx0'], out=Tx[0], in_=Eb[d], func=AF.Exp,
                              scale=sq_)
                            A('dve', 'tensor_tensor', R=['qs', 'Tx0'], W=['qd%d' % d], out=qd[d][:], in0=qs[:],
                              in1=Tx[0], op=ALU.mult)
                            A('act', 'activation', R=['Eb%d' % d], W=['Tx1'], out=Tx[1], in_=Eb[d], func=AF.Exp,
                              scale=sk_)
                            A('dve', 'tensor_tensor', R=['kk%d' % d, 'Tx1'], W=['kk%d' % d], out=kk[d][:],
                              in0=kk[d][:], in1=Tx[1], op=ALU.mult)

                        STOP('prep')
                        zero_in = {}
                        for d in range(2):
                            order = list(range(8)) if d == 0 else list(range(7, -1, -1))
                            cur = 0
                            s_zero = True
                            for oi, t in enumerate(order):
                                seq = t // tps
                                first = (t % tps == 0) if d == 0 else (t % tps == tps - 1)
                                last = (t % tps == tps - 1) if d == 0 else (t % tps == 0)
                                sl = oi % 2
                                A('pe', 'transpose', R=['kk%d' % d, 'ident_b'], W=['B30'],
                                  out=pbf[:, d * 2 + sl, :], in_=kk[d][:, t * 128:(t + 1) * 128], identity=ident_b[:])
                                A('act', 'copy', R=['B30'], W=['kt%d_%d' % (d, t)],
                                  out=kt[:, d, t, :], in_=pbf[:, d * 2 + sl, :])
                                A('pe', 'matmul', R=['kt%d_%d' % (d, t), 'vt%d' % t], W=['B21'],
                                  out=pads[:, 4 + d, :], lhsT=kt[:, d, t, :], rhs=vt[:, t, :], start=True, stop=True)
                                if first:
                                    if g == 0:
                                        s_zero = True
                                    else:
                                        s_zero = False
                                        DMA('sp', 'dma_start', W=['S%d_%d' % (d, cur)], out=Sst[d][cur][:],
                                            in_=st_hg[d, h, :, :])
                                zero_in[(d, t)] = s_zero
                                nxt = 1 - cur
                                wk_c = bnd[:, d, 2, t:t + 1]
                                wdec_c = bnd[:, d, 1, t:t + 1]
                                win_c = bnd[:, d, 0, t:t + 1]
                                if s_zero:
                                    A('dve', 'tensor_scalar', R=['B21', 'bnd%d' % d], W=['S%d_%d' % (d, nxt)],
                                      out=Sst[d][nxt][:], in0=pads[:, 4 + d, :], scalar1=wk_c, scalar2=None,
                                      op0=ALU.mult)
                                else:
                                    A('act', 'activation', R=['S%d_%d' % (d, cur), 'bnd%d' % d],
                                      W=['Ssc%d_%d' % (d, t)], out=Ssc[:, d, t, :], in_=Sst[d][cur][:], func=AF.Copy,
                                      scale=win_c)
                                    A('dve', 'tensor_scalar', R=['B21', 'bnd%d' % d], W=['stmp'], out=stmp[:],
                                      in0=pads[:, 4 + d, :], scalar1=wk_c, scalar2=None, op0=ALU.mult)
                                    A('dve', 'scalar_tensor_tensor', R=['S%d_%d' % (d, cur), 'stmp', 'bnd%d' % d],
                                      W=['S%d_%d' % (d, nxt)], out=Sst[d][nxt][:], in0=Sst[d][cur][:], scalar=wdec_c,
                                      in1=stmp[:], op0=ALU.mult, op1=ALU.add)
                                s_zero = False
                                cur = nxt
                                if last and g == 0:
                                    DMA('sp', 'dma_start', R=['S%d_%d' % (d, cur)], out=nhg[seq, d, h, :, :],
                                        in_=Sst[d][cur][:])

                        STOP('pass1')
                        for t in range(8):
                            sl = t % 2
                            for d in range(2):
                                A('pe', 'matmul', R=['kk%d' % d, 'qd%d' % d], W=['B20'], out=pads[:, d, :],
                                  lhsT=kk[d][:, t * 128:(t + 1) * 128], rhs=qd[d][:, t * 128:(t + 1) * 128],
                                  start=True, stop=True)
                            A('dve', 'tensor_tensor', R=['B20', 'msk'], W=['att%d' % sl], out=att[sl][:],
                              in0=pads[:, 0:2, :], in1=msk[:], op=ALU.mult)
                            mm = [(att[sl][:, 0, :], vt[:, t, :], ['att%d' % sl, 'vt%d' % t]),
                                  (att[sl][:, 1, :], vt[:, t, :], ['att%d' % sl, 'vt%d' % t])]
                            for d in range(2):
                                if not zero_in[(d, t)]:
                                    mm.append((qd[d][:, t * 128:(t + 1) * 128], Ssc[:, d, t, :],
                                               ['qd%d' % d, 'Ssc%d_%d' % (d, t)]))
                            for i, (l_, r_, rd) in enumerate(mm):
                                A('pe', 'matmul', R=rd, W=['B31'], inc=(i == len(mm) - 1), out=po[:, sl, :],
                                  lhsT=l_, rhs=r_, start=(i == 0), stop=(i == len(mm) - 1))
                            A('act', 'activation', R=['B31'], W=['sqj', 'rms%d' % sl], out=sqj[:],
                              in_=po[:, sl, :], func=AF.Square, accum_out=rms[:, sl, 0:1])
                            A('act', 'activation', R=['rms%d' % sl], W=['rms%d' % sl], out=rms[:, sl, 1:2],
                              in_=rms[:, sl, 0:1], func=AF.Sqrt, scale=1.0 / 128.0, bias=RMS_EPS)
                            A('dve', 'reciprocal', R=['rms%d' % sl], W=['rms%d' % sl], out=rms[:, sl, 2:3],
                              in_=rms[:, sl, 1:2])
                            A('dve', 'scalar_tensor_tensor', R=['B31', 'rms%d' % sl, 'gs%d' % t],
                              W=['og%d' % sl], out=ogt[sl][:], in0=po[:, sl, :], scalar=rms[:, sl, 2:3],
                              in1=gsb[:, t, :], op0=ALU.mult, op1=ALU.mult)
                            A('pe', 'transpose', R=['og%d' % sl, 'ident_b'], W=['B30'],
                              out=pbf[:, 4 + sl, :], in_=ogt[sl][:], identity=ident_b[:])
                            A('act', 'copy', R=['B30'], W=['mixT%d' % h],
                              out=mixT[:, h, t * 128:(t + 1) * 128], in_=pbf[:, 4 + sl, :])
                    P.barrier()
                    STOP('hg')

                    xr, gg, xc = Fb[0], Fb[1], Fb[2]
                    ra, ia, a2 = Fb[3], Fb[4], Fb[5]
                    xcb = Hb[0]
                    hh = [Fb[6], Fb[7]]
                    if g == 0:
                        L = 256
                    else:
                        L = 16

                    def perm_out(buf, hf):
                        if g == 0:
                            return buf[:, hf * 512:(hf + 1) * 512]
                        return buf.rearrange("p (c r) -> p r c", r=16)[:, hf * 8:(hf + 1) * 8, :]

                    def perm_in(ps_):
                        if g == 0:
                            return ps_
                        return ps_.rearrange("p (r c) -> p r c", c=64)

                    for c in range(4):
                        ub = c % 2
                        for hf in range(2):
                            ps_, tk = zproj_feat(ub, 0, hf)
                            A('act', 'activation', R=[tk], W=['xr'], out=perm_out(xr, hf), in_=perm_in(ps_),
                              func=AF.Copy)
                        for hf in range(2):
                            ps_, tk = zproj_feat(ub, 1, hf)
                            A('act', 'activation', R=[tk], W=['gg'], out=perm_out(gg, hf), in_=perm_in(ps_),
                              func=AF.Gelu_apprx_tanh)
                        if c < 3:
                            load_unit_weights(4 + c + 1, (c + 1) % 2)
                        A('pool', 'memset', W=['wg_st'], ap=wg_st[:], constant=0.0)
                        for gate in range(2):
                            for d in range(2):
                                for blk in range(2):
                                    DMA('sp', 'dma_start', W=['wg_st'],
                                        out=wg_st[blk * 64:(blk + 1) * 64, gate * 2 + d, blk * 64:(blk + 1) * 64],
                                        in_=rgw[gate, d, 2 * c + blk, :, :])
                        A('pool', 'tensor_copy', R=['wg_st'], W=['wgb'], out=wgb[:], in_=wg_st[:])
                        xv = xr.rearrange("p (s l) -> p s l", l=L)
                        cv = xc.rearrange("p (s l) -> p s l", l=L)

                        def cw(tap):
                            return ppt[:, PP_CW + tap * 4 + c:PP_CW + tap * 4 + c + 1]
                        A('dve', 'tensor_scalar', R=['xr', 'ppt'], W=['xc'], out=xc, in0=xr, scalar1=cw(2),
                          scalar2=ppt[:, PP_CB + c:PP_CB + c + 1], op0=ALU.mult, op1=ALU.add)
                        A('dve', 'scalar_tensor_tensor', R=['xr', 'ppt', 'xc'], W=['xc'], out=cv[:, :, 2:L],
                          in0=xv[:, :, 0:L - 2], scalar=cw(0), in1=cv[:, :, 2:L], op0=ALU.mult, op1=ALU.add)
                        A('dve', 'scalar_tensor_tensor', R=['xr', 'ppt', 'xc'], W=['xc'], out=cv[:, :, 1:L],
                          in0=xv[:, :, 0:L - 1], scalar=cw(1), in1=cv[:, :, 1:L], op0=ALU.mult, op1=ALU.add)
                        A('dve', 'scalar_tensor_tensor', R=['xr', 'ppt', 'xc'], W=['xc'], out=cv[:, :, 0:L - 1],
                          in0=xv[:, :, 1:L], scalar=cw(3), in1=cv[:, :, 0:L - 1], op0=ALU.mult, op1=ALU.add)
                        A('act', 'copy', R=['xc'], W=['xcb'], out=xcb[:], in_=xc)
                        for d in range(2):
                            for gate, dst, bcol in ((0, ra, PP_BR), (1, ia, PP_BI)):
                                for hf in range(2):
                                    pi = zc[0] % 2
                                    zc[0] += 1
                                    A('pe', 'matmul', R=['wgb', 'xcb'], W=['B1%d' % pi], out=pz[pi],
                                      lhsT=wgb[:, gate * 2 + d, :], rhs=xcb[:, hf * 512:(hf + 1) * 512],
                                      start=True, stop=True)
                                    A('act', 'activation', R=['B1%d' % pi, 'ppt'], W=['rg_%d' % gate],
                                      out=dst[:, hf * 512:(hf + 1) * 512], in_=pz[pi], func=AF.Sigmoid,
                                      bias=ppt[:, bcol + d * 4 + c:bcol + d * 4 + c + 1])
                            ca_c = rgc[:, 0, d * 4 + c:d * 4 + c + 1]
                            ca2_c = rgc[:, 1, d * 4 + c:d * 4 + c + 1]
                            A('act', 'activation', R=['rg_0', 'rgc1'], W=['a2'], out=a2, in_=ra, func=AF.Exp, scale=ca2_c)
                            A('act', 'activation', R=['rg_0', 'rgc0'], W=['rg_0'], out=ra, in_=ra, func=AF.Exp,
                              scale=ca_c)
                            A('act', 'activation', R=['a2'], W=['a2'], out=a2, in_=a2, func=AF.Sqrt, scale=-1.0, bias=1.0)
                            A('dve', 'tensor_tensor', R=['a2', 'rg_1'], W=['a2'], out=a2, in0=a2, in1=ia, op=ALU.mult)
                            A('dve', 'tensor_tensor', R=['a2', 'xc'], W=['a2'], out=a2, in0=a2, in1=xc, op=ALU.mult)
                            nscan = 4 if g == 0 else 1
                            Ls = 1024 // nscan
                            for s in range(nscan):
                                lo, hi = s * Ls, (s + 1) * Ls
                                if g == 0:
                                    init = 0.0
                                else:
                                    init = ppt[:, PP_SRG + d * 4 + c:PP_SRG + d * 4 + c + 1]
                                if d == 0:
                                    o_, a_, u_ = hh[0][:, lo:hi], ra[:, lo:hi], a2[:, lo:hi]
                                else:
                                    o_, a_, u_ = hh[1][:, lo:hi][:, ::-1], ra[:, lo:hi][:, ::-1], a2[:, lo:hi][:, ::-1]
                                A('dve', 'tensor_tensor_scan', R=['rg_0', 'a2', 'ppt'], W=['hh%d' % d], out=o_,
                                  data0=a_, data1=u_, initial=init, op0=ALU.mult, op1=ALU.add)
                                if g == 0:
                                    col = s * 8 + d * 4 + c
                                    src = hh[0][:, hi - 1:hi] if d == 0 else hh[1][:, lo:lo + 1]
                                    A('act', 'copy', R=['hh%d' % d], W=['rgout'], out=rgout[:, col:col + 1], in_=src)
                        A('dve', 'tensor_tensor', R=['hh0', 'hh1'], W=['hh0'], out=hh[0], in0=hh[0], in1=hh[1],
                          op=ALU.add)
                        if g == 0:
                            y_out = mixT[:, 4 + c, :]
                            y_h, y_g = hh[0], gg
                        else:
                            y_out = mixT[:, 4 + c, :].rearrange("p (r c) -> p c r", c=64)
                            y_h = hh[0].rearrange("p (c r) -> p c r", r=16)
                            y_g = gg.rearrange("p (c r) -> p c r", r=16)
                        A('dve', 'tensor_tensor', R=['hh0', 'gg'], W=['mixT%d' % (4 + c)], out=y_out, in0=y_h, in1=y_g,
                          op=ALU.mult)
                    STOP('rg')
                    if debug:
                        DMA('sp', 'dma_start', R=['mixT%d' % k for k in range(8)], out=dbg['mixT'][g, :, :],
                            in_=mixT[:].rearrange("p k t -> p (k t)"))
                    P.barrier()

                with ExitStack() as rs_:
                    with ExitStack() as p1s:
                        wo_b = sb("wo_b%d" % g, [128, 8, D], BF16, p1s)
                        wst2 = [sb("wst2_%d_%d" % (g, i), [128, 1024], F32, p1s) for i in range(2)]
                        dgb = [sb("dgb%d_%d" % (g, i), [128, 128], F32, p1s) for i in range(2)]
                        xres = [sb("xres%d_%d" % (g, i), [128, D], F32, p1s) for i in range(2)]
                        y1 = sb("y1_%d" % g, [128, D], F32, p1s)
                        stats = sb("stats%d" % g, [128, 2, 6], F32, p1s)
                        mv = sb("mv%d" % g, [128, 4], F32, p1s)
                        for k in range(8):
                            s = k % 2
                            DMA('sp', 'dma_start', W=['wst2_%d' % s], out=wst2[s][:], in_=w_out[k * 128:(k + 1) * 128, :])
                            if do_peer and g == groups[0]:
                                for ci in range(4 * k, 4 * k + 4):
                                    c, which = ci // 2, ci % 2
                                    src, co = ((pu, 0), (pv, D))[which]
                                    P.swdma('pool', 'dma_start', 'conv%d' % ci,
                                            out=uv16[c * 1024:(c + 1) * 1024, co:co + D], in_=src[c * 1024:(c + 1) * 1024, :])
                            A('act', 'copy', R=['wst2_%d' % s], W=['wo_b%d' % k], out=wo_b[:, k, :], in_=wst2[s][:])
                        for bi, jb in enumerate((16, 32, 24, 40)):
                            for k in range(8):
                                dsl = k % 2
                                A('dve', 'tensor_scalar', R=['ident_f', 'modT'], W=['dgb%d' % dsl], out=dgb[dsl][:],
                                  in0=ident_f[:], scalar1=modT[:, jb + k, cnd:cnd + 1], scalar2=None, op0=ALU.mult)
                                A('pe', 'matmul', R=['ones_f', 'dgb%d' % dsl], W=['B0%d' % (k // 4)],
                                  out=Q[0][:, k * 128:(k + 1) * 128], lhsT=ones_f[:, 0:128], rhs=dgb[dsl][:],
                                  start=True, stop=True)
                            A('act', 'copy', R=['B00', 'B01'], W=['bc%d' % bi], out=bc[bi][:], in_=Q[0][:])
                        for t in range(8):
                            xb = t % 2
                            DMA('sp', 'dma_start', W=['xres%d' % xb], out=xres[xb][:],
                                in_=xg[g][t * 128:(t + 1) * 128, :])
                            for hf in range(2):
                                for k in range(8):
                                    A('pe', 'matmul', R=['wo_b%d' % k], W=['B0%d' % hf], inc=(k == 7),
                                      out=Q[0][:, hf * 512:(hf + 1) * 512], lhsT=mixT[:, k, t * 128:(t + 1) * 128],
                                      rhs=wo_b[:, k, hf * 512:(hf + 1) * 512], start=(k == 0), stop=(k == 7))
                            A('dve', 'tensor_tensor', R=['B00', 'B01', 'bc0'], W=['y1'], out=y1[:], in0=Q[0][:], in1=bc[0][:],
                              op=ALU.mult)
                            A('dve', 'scalar_tensor_tensor', R=['xres%d' % xb, 'y1'], W=['y1'], out=y1[:],
                              in0=xres[xb][:], scalar=ALPHA, in1=y1[:], op0=ALU.mult, op1=ALU.add)
                            ln_block(A, y1, x1s[:, t, :], stats, mv, bvt, BV_L1G, BV_L1B, 'y1', 'x1s%d' % t)
                        STOP('p1a')
                        if debug:
                            DMA('sp', 'dma_start', R=['x1s%d' % t for t in range(8)], out=dbg['x1'][g, :, :],
                                in_=x1s[:].rearrange("p t d -> p (t d)"))
                        P.barrier()
                    mx_.close()

                    with ExitStack() as p1s:
                        wq_b = sb("wq_b%d" % g, [128, 8, 2048], BF16, p1s)
                        wst2 = [sb("wst3_%d_%d" % (g, i), [128, 512], F32, p1s) for i in range(2)]
                        h2_ = [sb("h2_%d_%d" % (g, i), [128, D], F32, p1s) for i in range(2)]
                        h2T_ = [sb("h2T%d_%d" % (g, i), [128, 8, 128], BF16, p1s) for i in range(2)]
                        qTb_ = [sb("qTb%d_%d" % (g, i), [128, 16, 128], BF16, p1s) for i in range(2)]
                        iota_ps = Q[3][:, 0:16]
                        sif_ps = Q[3][:, 16:272].rearrange("p (a k) -> p a k", k=16)
                        scs = sb("scs%d" % g, [128, 16, 128], F32, p1s)
                        scw = [sb("scw%d_%d" % (g, i), [128, 128], F32, p1s) for i in range(16)]
                        sv = sb("sv%d" % g, [128, 16, 16], F32, p1s)
                        si = sb("si%d" % g, [128, 16, 16], U32, p1s)
                        cand = sb("cand%d" % g, [128, 8, 256], F32, p1s)
                        cwk = [sb("cwk%d_%d" % (g, i), [128, 256], F32, p1s) for i in range(8)]
                        fv = sb("fv%d" % g, [128, 8, 16], F32, p1s)
                        fi = sb("fi%d" % g, [128, 8, 16], U32, p1s)
                        fab = sb("fab%d" % g, [128, 2, 128], U32, p1s)
                        fabf = sb("fabf%d" % g, [128, 2, 128], F32, p1s)
                        oh_ = [sb("oh%d_%d" % (g, i), [128, 8, 256], BF16, p1s) for i in range(2)]
                        isel = sb("isel%d" % g, [128, 2, 128], F32, p1s)
                        idxf = sb("idxf%d" % g, [128, 128], F32, p1s)
                        gex = sb("gex%d" % g, [128, 8, 16], F32, p1s)
                        gsm = sb("gsm%d" % g, [128, 2, 8], F32, p1s)
                        ws = 0
                        for k in range(8):
                            for cb in range(4):
                                s = ws % 2
                                ws += 1
                                DMA('sp', 'dma_start', W=['wst3_%d' % s], out=wst2[s][:],
                                    in_=wq[k * 128:(k + 1) * 128, cb * 512:(cb + 1) * 512])
                                if s == 0:
                                    A('pool', 'tensor_copy', R=['wst3_%d' % s], W=['wq_b%d' % k],
                                      out=wq_b[:, k, cb * 512:(cb + 1) * 512], in_=wst2[s][:])
                                else:
                                    A('act', 'copy', R=['wst3_%d' % s], W=['wq_b%d' % k],
                                      out=wq_b[:, k, cb * 512:(cb + 1) * 512], in_=wst2[s][:])
                        A('dve', 'tensor_copy', R=['iota16'], W=['B30'], out=iota_ps, in_=iota16[:])
                        def p1b_front(t):
                            tb = t % 2
                            h2, h2T, qTb = h2_[tb], h2T_[tb], qTb_[tb]
                            h2eng = 'dve' if (do_peer and g == groups[0]) else 'pool'
                            A(h2eng, 'tensor_tensor', R=['bc1'], W=['h2_%d' % tb], out=h2[:], in0=x1s[:, t, :], in1=bc[1][:],
                              op=ALU.mult)
                            A(h2eng, 'tensor_tensor', R=['h2_%d' % tb, 'bc2'], W=['h2_%d' % tb], out=h2[:], in0=h2[:],
                              in1=bc[2][:], op=ALU.add)
                            for k in range(8):
                                A('pe', 'transpose', R=['h2_%d' % tb, 'ident_f'], W=['B0%d' % (k // 4)],
                                  out=Q[0][:, k * 128:(k + 1) * 128], in_=h2[:, k * 128:(k + 1) * 128], identity=ident_f[:])
                            A('act', 'copy', R=['B00', 'B01'], W=['h2T%d' % tb], out=h2T[:].rearrange("p k t -> p (k t)"),
                              in_=Q[0][:])
                            for hp in range(16):
                                qq = Q[1 + hp // 8]
                                for k in range(8):
                                    A('pe', 'matmul', R=['wq_b%d' % k, 'h2T%d' % tb], W=['B%d%d' % (1 + hp // 8, (hp % 8) // 4)],
                                      inc=(k == 7 and hp % 4 == 3), out=qq[:, (hp % 8) * 128:(hp % 8 + 1) * 128],
                                      lhsT=wq_b[:, k, hp * 128:(hp + 1) * 128], rhs=h2T[:, k, :],
                                      start=(k == 0), stop=(k == 7))
                            A('act', 'copy', R=['B10', 'B11'], W=['qTb%d_0' % tb],
                              out=qTb[:, 0:8, :].rearrange("p a t -> p (a t)"), in_=Q[1][:])
                            A('act', 'copy', R=['B20', 'B21'], W=['qTb%d_1' % tb],
                              out=qTb[:, 8:16, :].rearrange("p a t -> p (a t)"), in_=Q[2][:])
                            for hp in range(16):
                                qq = Q[1 + hp // 8]
                                A('pe', 'matmul', R=['qTb%d_%d' % (tb, hp // 8), 'keysT'],
                                  W=['B%d%d' % (1 + hp // 8, (hp % 8) // 4)],
                                  inc=(hp % 4 == 3), out=qq[:, (hp % 8) * 128:(hp % 8 + 1) * 128], lhsT=qTb[:, hp, :],
                                  rhs=keysT[:, hp, :], start=True, stop=True)
                        def p1b_mid(t):
                            A('act', 'copy', R=['B10', 'B11'], W=['scs0'], out=scs[:, 0:8, :].rearrange("p a t -> p (a t)"),
                              in_=Q[1][:])
                            A('act', 'copy', R=['B20', 'B21'], W=['scs1'], out=scs[:, 8:16, :].rearrange("p a t -> p (a t)"),
                              in_=Q[2][:])
                            for h0 in range(0, 16, 16):
                                hps = range(h0, h0 + 16)
                                for hp in hps:
                                    A('dve', 'max', R=['scs%d' % (hp // 8)], W=['sv%d' % hp], out=sv[:, hp, 0:8],
                                      in_=scs[:, hp, :])
                                for hp in hps:
                                    A('dve', 'max_index', R=['scs%d' % (hp // 8), 'sv%d' % hp], W=['si%d' % hp],
                                      out=si[:, hp, 0:8], in_max=sv[:, hp, 0:8], in_values=scs[:, hp, :])
                                for hp in hps:
                                    A('dve', 'match_replace', R=['scs%d' % (hp // 8), 'sv%d' % hp], W=['scw%d' % (hp % 16)],
                                      out=scw[hp % 16][:], in_to_replace=sv[:, hp, 0:8], in_values=scs[:, hp, :],
                                      imm_value=-1e30)
                                for hp in hps:
                                    A('dve', 'max', R=['scw%d' % (hp % 16)], W=['svb%d' % hp], out=sv[:, hp, 8:16],
                                      in_=scw[hp % 16][:])
                                for hp in hps:
                                    A('dve', 'max_index', R=['scw%d' % (hp % 16), 'svb%d' % hp], W=['sib%d' % hp],
                                      out=si[:, hp, 8:16], in_max=sv[:, hp, 8:16], in_values=scw[hp % 16][:])
                        def p1b_back(t):
                            svt = ['sv%d' % hp for hp in range(16)] + ['svb%d' % hp for hp in range(16)]
                            svv = sv[:].rearrange("p (h q) k -> p h q k", q=2)
                            A('dve', 'tensor_tensor', R=svt, W=['cand'],
                              out=cand[:].rearrange("p h (a b) -> p h a b", b=16),
                              in0=svv[:, :, 0, :].unsqueeze(3).to_broadcast([128, 8, 16, 16]),
                              in1=svv[:, :, 1, :].unsqueeze(2).to_broadcast([128, 8, 16, 16]), op=ALU.add)
                            for h0 in range(0, 8, 8):
                                hs_ = range(h0, h0 + 8)
                                for h_ in hs_:
                                    A('dve', 'max', R=['cand'], W=['fv%d' % h_], out=fv[:, h_, 0:8], in_=cand[:, h_, :])
                                for h_ in hs_:
                                    A('dve', 'max_index', R=['cand', 'fv%d' % h_], W=['fi%d' % h_], out=fi[:, h_, 0:8],
                                      in_max=fv[:, h_, 0:8], in_values=cand[:, h_, :])
                                for h_ in hs_:
                                    A('dve', 'match_replace', R=['cand', 'fv%d' % h_], W=['cwk%d' % (h_ % 8)],
                                      out=cwk[h_ % 8][:], in_to_replace=fv[:, h_, 0:8], in_values=cand[:, h_, :],
                                      imm_value=-1e30)
                                for h_ in hs_:
                                    A('dve', 'max', R=['cwk%d' % (h_ % 8)], W=['fvb%d' % h_], out=fv[:, h_, 8:16],
                                      in_=cwk[h_ % 8][:])
                                for h_ in hs_:
                                    A('dve', 'max_index', R=['cwk%d' % (h_ % 8), 'fvb%d' % h_], W=['fib%d' % h_],
                                      out=fi[:, h_, 8:16], in_max=fv[:, h_, 8:16], in_values=cwk[h_ % 8][:])
                            fvt = ['fv%d' % h_ for h_ in range(8)] + ['fvb%d' % h_ for h_ in range(8)]
                            fif = fi[:].rearrange("p h k -> p (h k)")
                            A('dve', 'tensor_single_scalar', R=['fi%d' % h_ for h_ in range(8)] + ['fib%d' % h_ for h_ in range(8)], W=['fab0'], out=fab[:, 0, :], in_=fif, scalar=4,
                              op=ALU.logical_shift_right)
                            A('dve', 'tensor_single_scalar', R=['fi%d' % h_ for h_ in range(8)] + ['fib%d' % h_ for h_ in range(8)], W=['fab1'], out=fab[:, 1, :], in_=fif, scalar=15,
                              op=ALU.bitwise_and)
                            A('dve', 'tensor_copy', R=['fab0', 'fab1'], W=['fabf'], out=fabf[:], in_=fab[:])
                            A('dve', 'tensor_copy', R=['si%d' % hp for hp in range(16)] + ['sib%d' % hp for hp in range(16)], W=['B30'], out=sif_ps, in_=si[:])
                            sifv = sif_ps.rearrange("p (h q) k -> p h q k", q=2)
                            ohv = [o_[:].rearrange("p h (k a) -> p h k a", a=16) for o_ in oh_]
                            for q_ in range(2):
                                A('dve', 'tensor_tensor', R=['fabf', 'B30'], W=['oh%d' % q_], out=ohv[q_],
                                  in0=fabf[:, q_, :].rearrange("p (h k) -> p h k", k=16).unsqueeze(3).to_broadcast(
                                      [128, 8, 16, 16]),
                                  in1=iota_ps.unsqueeze(1).unsqueeze(1).to_broadcast([128, 8, 16, 16]),
                                  op=ALU.is_equal)
                            for q_ in range(2):
                                A('dve', 'tensor_tensor', R=['oh%d' % q_, 'B30'], W=['oh%d' % q_], out=ohv[q_], in0=ohv[q_],
                                  in1=sifv[:, :, q_, :].unsqueeze(2).to_broadcast([128, 8, 16, 16]), op=ALU.mult)
                            for q_ in range(2):
                                A('dve', 'tensor_reduce', R=['oh%d' % q_], W=['isel%d' % q_], out=isel[:, q_, :],
                                  in_=oh_[q_][:].rearrange("p h (k a) -> p (h k) a", a=16), axis=AX.X, op=ALU.add)
                            A('dve', 'scalar_tensor_tensor', R=['isel0', 'isel1'], W=['idxf'], out=idxf[:],
                              in0=isel[:, 0, :], scalar=128.0, in1=isel[:, 1, :], op0=ALU.mult, op1=ALU.add)
                            A('dve', 'tensor_copy', R=['idxf'], W=['idx%d' % t], out=idx_all[:, t, :], in_=idxf[:])
                            A('dve', 'tensor_tensor', R=fvt, W=['gex'], out=gex[:], in0=fv[:],
                              in1=fv[:, :, 0:1].to_broadcast([128, 8, 16]), op=ALU.subtract)
                            A('act', 'activation', R=['gex'], W=['gex'], out=gex[:], in_=gex[:], func=AF.Exp)
                            A('dve', 'tensor_reduce', R=['gex'], W=['gsm'], out=gsm[:, 0, :], in_=gex[:], axis=AX.X,
                              op=ALU.add)
                            A('dve', 'reciprocal', R=['gsm'], W=['gsm'], out=gsm[:, 1, :], in_=gsm[:, 0, :])
                            A('dve', 'tensor_tensor', R=['gex', 'gsm'], W=['gate%d' % t],
                              out=gate_all[:, t, :].rearrange("p (h k) -> p h k", k=16), in0=gex[:],
                              in1=gsm[:, 1, :].unsqueeze(2).to_broadcast([128, 8, 16]), op=ALU.mult)
                        p1b_front(0)
                        for t in range(8):
                            p1b_mid(t)
                            if t + 1 < 8:
                                p1b_front(t + 1)
                            p1b_back(t)
                        if debug:
                            DMA('sp', 'dma_start', R=['idx%d' % t for t in range(8)], out=dbg['idx'][g, :, :],
                                in_=idx_all[:].rearrange("p t d -> p (t d)"))
                            DMA('sp', 'dma_start', R=['gate%d' % t for t in range(8)], out=dbg['gate'][g, :, :],
                                in_=gate_all[:].rearrange("p t d -> p (t d)"))
                        P.barrier()

                    with ExitStack() as p2s:
                        NV, ND, GJ = 22, 8, 4
                        vb32 = [sb("vb%d_%d" % (g, i), [128, D], F32, p2s) for i in range(NV)]
                        vb_ = [x[:].bitcast(BF16) for x in vb32]
                        dg_ = [sb("dg%d_%d" % (g, i), [128, 128], BF16, p2s) for i in range(ND)]
                        junk = [sb("junk%d_%d" % (g, i), [128, D], BF16, p2s) for i in range(4)]
                        apre = [sb("apre%d_%d" % (g, i), [128, 128], F32, p2s) for i in range(2)]
                        gact = [sb("gact%d_%d" % (g, i), [128, 128], F32, p2s) for i in range(2)]
                        htmp = sb("htmp%d" % g, [128, D], F32, p2s)
                        y2 = sb("y2_%d" % g, [128, D], F32, p2s)
                        outt = [sb("outt%d_%d" % (g, i), [128, D], F32, p2s) for i in range(2)]
                        stats = sb("stats2_%d" % g, [128, 2, 6], F32, p2s)
                        mv = sb("mv2_%d" % g, [128, 4], F32, p2s)
                        ph2 = [Q[0], Q[1]]
                        pvv = [Q[2], Q[3]]
                        import os as _os
                        ntiles = int(_os.environ.get('K_NT', '8')) if do_peer else 0
                        NJ = int(_os.environ.get('K_NJ', '128'))
                        if do_peer:
                            P.wait_sw('pool', ['conv%d' % i for i in range(32)])
                        for t in range(ntiles):
                            pb = t % 2
                            A('dve', 'tensor_tensor', R=['bc1'], W=['htmp'], out=htmp[:], in0=x1s[:, t, :], in1=bc[1][:],
                              op=ALU.mult)
                            A('dve', 'tensor_tensor', R=['htmp', 'bc2'], W=['ph2_%d' % pb], out=ph2[pb][:], in0=htmp[:],
                              in1=bc[2][:], op=ALU.add)
                            for j in range(NJ):
                                vs = j % NV
                                P.swdma('pool', 'indirect_dma_start', 'vb%d' % vs, W=['vb%d' % vs], out=vb32[vs][:],
                                        out_offset=None, in_=uv32[:, :],
                                        in_offset=bass.IndirectOffsetOnAxis(ap=idx_all[:, t, j:j + 1], axis=0))
                                A('dve', 'scalar_tensor_tensor', R=['vb%d' % vs, 'ph2_%d' % pb],
                                  W=['apre%d_%d' % (pb, j), 'junk%d' % (j % 4)], out=junk[j % 4][:], in0=vb_[vs][:, 0:D],
                                  scalar=1.0, in1=ph2[pb][:], op0=ALU.mult, op1=ALU.mult, accum_out=apre[pb][:, j:j + 1])
                                if j % GJ == GJ - 1:
                                    jg = j - GJ + 1
                                    gt = 'gact%d_%d' % (pb, j // GJ)
                                    A('act', 'activation', R=['apre%d_%d' % (pb, jj) for jj in range(jg, j + 1)],
                                      W=[gt], out=gact[pb][:, jg:jg + GJ], in_=apre[pb][:, jg:jg + GJ],
                                      func=AF.Gelu_apprx_tanh)
                                    A('dve', 'tensor_tensor', R=[gt], W=[gt], out=gact[pb][:, jg:jg + GJ],
                                      in0=gact[pb][:, jg:jg + GJ], in1=gate_all[:, t, jg:jg + GJ], op=ALU.mult)
                                    for jj in range(jg, j + 1):
                                        vs2, ds2 = jj % NV, jj % ND
                                        A('act', 'activation', R=[gt], W=['dg%d' % ds2], out=dg_[ds2][:], in_=ident_f[:],
                                          func=AF.Copy, scale=gact[pb][:, jj:jj + 1])
                                        for hf in range(2):
                                            A('pe', 'matmul', R=['dg%d' % ds2, 'vb%d' % vs2], W=['B%d%d' % (2 + pb, hf)],
                                              inc=(hf == 1), out=pvv[pb][:, hf * 512:(hf + 1) * 512], lhsT=dg_[ds2][:],
                                              rhs=vb_[vs2][:, D + hf * 512:D + (hf + 1) * 512], start=(jj == 0),
                                              stop=(jj == NJ - 1))
                            A('dve', 'tensor_tensor', R=['B%d0' % (2 + pb), 'B%d1' % (2 + pb), 'bc3'], W=['y2'], out=y2[:], in0=pvv[pb][:],
                              in1=bc[3][:], op=ALU.mult)
                            A('dve', 'scalar_tensor_tensor', R=['y2'], W=['y2'], out=y2[:], in0=x1s[:, t, :],
                              scalar=ALPHA, in1=y2[:], op0=ALU.mult, op1=ALU.add)
                            ob = t % 2
                            ln_block(A, y2, outt[ob][:], stats, mv, bvt, BV_L2G, BV_L2B, 'y2', 'outt%d' % ob)
                            DMA('sp', 'dma_start', R=['outt%d' % ob], out=yg[g][t * 128:(t + 1) * 128, :],
                                in_=outt[ob][:])
                        P.barrier()
        DMA('sp', 'dma_start', R=['rgout'], out=nrg[:, :], in_=rgout[:])
        P.finish()
        P.emit()


def make_in_maps(inp, ncores=NCORES):
    f = lambda a: np.ascontiguousarray(np.asarray(a, dtype=np.float32))
    x_prompt, x_sample = f(inp['x_prompt']), f(inp['x_sample'])
    c, c_ctx = f(inp['c']), f(inp['c_ctx'])
    st_hg, st_rg = f(inp['state_hgrn']), f(inp['state_rglru'])
    bv = np.concatenate([f(inp['hgrn_norm_g'])[0], f(inp['ln1_g'])[0], f(inp['ln1_b'])[0],
                         f(inp['ln2_g'])[0], f(inp['ln2_b'])[0]]).astype(np.float32)
    rgw = np.ascontiguousarray(np.stack([f(inp['rg_wr'])[0], f(inp['rg_wi'])[0]], axis=0))
    keys = np.ascontiguousarray(f(inp['peer_keys'])[0].reshape(16, 128, 128))
    shared = {
        'bv': bv, 'w_ada': f(inp['w_ada'])[0], 'w_in': f(inp['w_in'])[0], 'w_out': f(inp['w_out'])[0],
        'peer_wq': f(inp['peer_wq'])[0], 'peer_keys': keys, 'rgw': rgw,
        'peer_u': f(inp['peer_u'])[0], 'peer_v': f(inp['peer_v'])[0],
    }

    def pcol(v, n):
        return v.reshape(n, 128).T

    maps = []
    for cid in range(ncores):
        pp = np.zeros((128, NPP), np.float32)
        cond = np.stack([c_ctx, c[cid]], axis=0)
        pp[:, PP_COND:PP_COND + 16] = cond.reshape(2, 8, 128).transpose(2, 1, 0).reshape(128, 16)
        pp[:, PP_BADA:PP_BADA + 48] = pcol(f(inp['b_ada'])[0], 48)
        lb = f(inp['hgrn_lb'])
        pp[:, PP_LB:PP_LB + 16] = lb.reshape(2, 2, 4, 128).transpose(3, 0, 1, 2).reshape(128, 16)
        pp[:, PP_CW:PP_CW + 16] = f(inp['conv_w'])[0].reshape(4, 4, 128).transpose(2, 0, 1).reshape(128, 16)
        pp[:, PP_CB:PP_CB + 4] = pcol(f(inp['conv_b'])[0], 4)
        pp[:, PP_BR:PP_BR + 8] = f(inp['rg_br'])[0].reshape(2, 4, 128).transpose(2, 0, 1).reshape(128, 8)
        pp[:, PP_BI:PP_BI + 8] = f(inp['rg_bi'])[0].reshape(2, 4, 128).transpose(2, 0, 1).reshape(128, 8)
        pp[:, PP_LAM:PP_LAM + 8] = f(inp['rg_lam'])[0].reshape(2, 4, 128).transpose(2, 0, 1).reshape(128, 8)
        pp[:, PP_SRG:PP_SRG + 8] = st_rg[cid, 0].reshape(2, 4, 128).transpose(2, 0, 1).reshape(128, 8)
        m = dict(shared)
        m['xp'] = np.ascontiguousarray(x_prompt[4 * cid:4 * cid + 4].reshape(1024, D))
        m['xs'] = np.ascontiguousarray(x_sample[cid])
        m['pp'] = pp
        m['st_hg'] = np.ascontiguousarray(st_hg[cid, 0])
        maps.append(m)
    return maps


def assemble(rs):
    y_prompt = np.concatenate([r['yp'].reshape(4, 256, D) for r in rs], axis=0).astype(np.float32)
    y_sample = np.stack([r['ys'] for r in rs], axis=0).astype(np.float32)
    new_hg = np.concatenate([r['nhg'].reshape(4, 1, 2, 4, 128, 128) for r in rs], axis=0).astype(np.float32)
    new_rg = np.concatenate(
        [r['nrg'].reshape(128, 4, 2, 4).transpose(1, 2, 3, 0).reshape(4, 1, 2, 512) for r in rs], axis=0
    ).astype(np.float32)
    return (y_prompt, y_sample, new_hg, new_rg)


_NC_CACHE = {}


def kernel(**inputs):
    if 'nc' not in _NC_CACHE:
        _NC_CACHE['nc'] = build_program()
    nc = _NC_CACHE['nc']
    maps = make_in_maps(inputs)
    res = run_bass_kernel_spmd(nc, maps, core_ids=list(range(NCORES)))
    return assemble(res.results)
```

```python
from contextlib import ExitStack

import numpy as np
import concourse.bass as bass
import concourse.mybir as mybir
from concourse.bass_utils import run_bass_kernel_spmd

F32 = mybir.dt.float32
BF16 = mybir.dt.bfloat16
I32 = mybir.dt.int32
U32 = mybir.dt.uint32
AF = mybir.ActivationFunctionType
ALU = mybir.AluOpType
AX = mybir.AxisListType

D = 1024
NCORES = 8
ALPHA = 2.0 ** 0.25
LN_EPS = 1e-5
RMS_EPS = 1e-6

PP_COND = 0
PP_BADA = 16
PP_LB = 64
PP_CW = 80
PP_CB = 96
PP_BR = 100
PP_BI = 108
PP_LAM = 116
PP_SRG = 124
NPP = 132
BV_NG, BV_L1G, BV_L1B, BV_L2G, BV_L2B, BVN = 0, 512, 1536, 2560, 3584, 4608


class Prog:
    ENG = ('pe', 'act', 'dve', 'pool', 'sp')

    def __init__(self, nc, es, n_dma_sems=28):
        self.nc = nc
        self.q = {e: [] for e in self.ENG}
        self.sem = {e: es.enter_context(nc.semaphore('s_' + e)) for e in self.ENG}
        self.cnt = {e: 0 for e in self.ENG}
        self.pending = {e: False for e in self.ENG}
        self.waited = {e: {} for e in self.ENG}
        self.last_w = {}
        self.readers = {}
        self.dsem = [es.enter_context(nc.semaphore('d%d' % i)) for i in range(n_dma_sems)]
        self.dval = [0] * n_dma_sems
        self.drr = 0
        self.nops = 0
        self.swsem = {}
        self.swsems = []
        self.swgen = []
        self._es = es

    def _semh(self, key):
        if isinstance(key, str):
            return self.sem[key]
        if key[0] == 'sw':
            return self.swsems[key[1]]
        return self.dsem[key[1]]

    def swdma(self, e, meth, slot, R=(), W=(), **kw):
        if slot not in self.swsem:
            self.swsem[slot] = len(self.swsems)
            self.swsems.append(self._es.enter_context(self.nc.semaphore('w%d' % len(self.swsems))))
            self.swgen.append(0)
        i = self.swsem[slot]
        deps = self._deps(R, W)
        if self.swgen[i] > 0:
            deps.append(((('sw', i), self.swgen[i]), True))
        self._emit_waits(e, deps)
        self.swgen[i] += 16
        key = (('sw', i), self.swgen[i])
        self.q[e].append(('swdma', meth, kw, i))
        self._record(key, R, W)
        self.nops += 1

    def _deps(self, R, W):
        deps = []
        for t in R:
            if t in self.last_w:
                deps.append((self.last_w[t], True))
        for t in W:
            if t in self.last_w:
                deps.append((self.last_w[t], False))
            for k in self.readers.get(t, ()):
                deps.append((k, False))
        return deps

    def _emit_waits(self, e, deps):
        need = {}
        for ((k, v), raw) in deps:
            if k == e and e == 'pe':
                continue
            if self.waited[e].get(k, 0) >= v:
                continue
            if need.get(k, 0) < v:
                need[k] = v
        for k, v in need.items():
            self.waited[e][k] = v
            self.q[e].append(('wait', k, v))

    def _record(self, key, R, W):
        for t in W:
            self.last_w[t] = key
            self.readers[t] = []
        for t in R:
            self.readers.setdefault(t, []).append(key)

    def op(self, e, meth, R=(), W=(), inc=True, **kw):
        W = list(W) + [t for t in R if t[0] == 'B' and t not in W]
        self._emit_waits(e, self._deps(R, W))
        if inc:
            self.cnt[e] += 1
            v = self.cnt[e]
            self.pending[e] = False
            self.q[e].append(('op', meth, kw, e))
        else:
            v = self.cnt[e] + 1
            self.pending[e] = True
            self.q[e].append(('op', meth, kw, None))
        self._record((e, v), R, W)
        self.nops += 1

    def dma(self, e, meth, R=(), W=(), **kw):
        i = self.drr % len(self.dsem)
        self.drr += 1
        deps = self._deps(R, W)
        if self.dval[i] > 0:
            deps.append(((('dma', i), self.dval[i]), True))
        self._emit_waits(e, deps)
        self.dval[i] += 16
        key = (('dma', i), self.dval[i])
        self.q[e].append(('dma', meth, kw, i))
        self._record(key, R, W)
        self.nops += 1

    def wait_sw(self, e, slots):
        deps = [((('sw', self.swsem[sl]), self.swgen[self.swsem[sl]]), True) for sl in slots if sl in self.swsem]
        self._emit_waits(e, deps)

    def barrier(self):
        for en in ('pe', 'act', 'dve', 'pool'):
            assert not self.pending[en], en
        for e in self.ENG:
            deps = [((e2, self.cnt[e2]), True) for e2 in self.ENG if e2 != e and self.cnt[e2] > 0]
            deps += [((('dma', i), v), True) for i, v in enumerate(self.dval) if v > 0]
            self._emit_waits(e, deps)
        self.last_w = {}
        self.readers = {}

    def finish(self, e='sp'):
        for en in ('pe', 'act', 'dve', 'pool'):
            assert not self.pending[en], en
        deps = [((('dma', i), v), True) for i, v in enumerate(self.dval) if v > 0]
        deps += [((e2, self.cnt[e2]), True) for e2 in self.ENG if e2 != e and self.cnt[e2] > 0]
        self._emit_waits(e, deps)

    def emit(self):
        nc = self.nc
        with nc.Block() as block:
            def run(e, eng):
                for item in self.q[e]:
                    if item[0] == 'wait':
                        eng.wait_ge(self._semh(item[1]), item[2])
                    elif item[0] == 'op':
                        ins = getattr(eng, item[1])(**item[2])
                        if item[3] is not None:
                            ins.then_inc(self.sem[item[3]], 1)
                    elif item[0] == 'clear':
                        eng.sem_clear(self.swsems[item[1]])
                    elif item[0] == 'swdma':
                        ins = getattr(eng, item[1])(**item[2])
                        ins.then_inc(self.swsems[item[3]], 16)
                    else:
                        ins = getattr(eng, item[1])(**item[2])
                        ins.then_inc(self.dsem[item[3]], 16)

            @block.tensor
            def _(eng):
                run('pe', eng)

            @block.scalar
            def _(eng):
                run('act', eng)

            @block.vector
            def _(eng):
                run('dve', eng)

            @block.gpsimd
            def _(eng):
                run('pool', eng)

            @block.sync
            def _(eng):
                run('sp', eng)


def ln_block(A, src, dst_ap, stats, mv, bvt, og, ob, src_tok, dst_tok):
    for hf in range(2):
        A('dve', 'bn_stats', R=[src_tok], W=['ln_stats'], out=stats[:, hf, :], in_=src[:, hf * 512:(hf + 1) * 512])
    A('dve', 'bn_aggr', R=['ln_stats'], W=['ln_mv'], out=mv[:, 0:2], in_=stats[:].rearrange("p a b -> p (a b)"))
    A('act', 'activation', R=['ln_mv'], W=['ln_mv'], out=mv[:, 2:3], in_=mv[:, 1:2], func=AF.Sqrt, scale=1.0,
      bias=LN_EPS)
    A('dve', 'reciprocal', R=['ln_mv'], W=['ln_mv'], out=mv[:, 2:3], in_=mv[:, 2:3])
    A('dve', 'tensor_scalar', R=['ln_mv'], W=['ln_mv'], out=mv[:, 3:4], in0=mv[:, 0:1], scalar1=mv[:, 2:3],
      scalar2=-1.0, op0=ALU.mult, op1=ALU.mult)
    A('act', 'activation', R=['ln_mv', src_tok], W=[src_tok], out=src[:], in_=src[:], func=AF.Identity,
      scale=mv[:, 2:3], bias=mv[:, 3:4])
    A('dve', 'tensor_tensor', R=[src_tok, 'bvt'], W=[src_tok], out=src[:], in0=src[:], in1=bvt[:, og:og + 1024],
      op=ALU.mult)
    A('dve', 'tensor_tensor', R=[src_tok, 'bvt'], W=[dst_tok], out=dst_ap, in0=src[:], in1=bvt[:, ob:ob + 1024],
      op=ALU.add)


class _StopBuild(Exception):
    pass


def build_program(groups=(0, 1), do_peer=True, debug=False, stop_at=None):
    nc = bass.Bass("TRN2", target_bir_lowering=False)
    try:
        _build(nc, groups, do_peer, debug, stop_at)
    except _StopBuild:
        pass
    return nc


def _build(nc, groups, do_peer, debug, stop_at):

    def din(name, shape, dt=F32):
        return nc.dram_tensor(name, shape, dt, kind="ExternalInput").ap()

    def dout(name, shape, dt=F32):
        return nc.dram_tensor(name, shape, dt, kind="ExternalOutput").ap()

    xg = [din("xp", [1024, D]), din("xs", [1024, D])]
    pp = din("pp", [128, NPP])
    bvd = din("bv", [BVN])
    st_hg = din("st_hg", [2, 4, 128, 128])
    w_ada = din("w_ada", [D, 6 * D])
    w_in = din("w_in", [D, 3584])
    w_out = din("w_out", [D, D])
    wq = din("peer_wq", [D, 2048])
    keys = din("peer_keys", [16, 128, 128])
    rgw = din("rgw", [2, 2, 8, 64, 64])
    pu = din("peer_u", [16384, D])
    pv = din("peer_v", [16384, D])
    uv32 = nc.dram_tensor("uv32", [16384, D], F32, kind="Internal").ap()
    uv16 = uv32.bitcast(BF16)
    yg = [dout("yp", [1024, D]), dout("ys", [1024, D])]
    nhg = dout("nhg", [4, 2, 4, 128, 128])
    nrg = dout("nrg", [128, 32])
    dbg = {}
    if debug:
        dbg['modT'] = dout("d_modT", [128, 96])
        dbg['mixT'] = dout("d_mixT", [2, 128, 8 * 1024], BF16)
        dbg['x1'] = dout("d_x1", [2, 128, 8 * 1024])
        dbg['idx'] = dout("d_idx", [2, 128, 1024], I32)
        dbg['gate'] = dout("d_gate", [2, 128, 1024])

    with ExitStack() as es:
        P = Prog(nc, es)

        def sb(name, shape, dt=F32, scope=es):
            return scope.enter_context(nc.sbuf_tensor(name, shape, dt))

        A = P.op
        DMA = P.dma

        def STOP(name):
            if stop_at == name:
                P.finish()
                P.emit()
                raise _StopBuild()

        Q = [es.enter_context(nc.psum_tensor("Q%d" % i, [128, 1024], F32)) for i in range(4)]
        ident_f = sb("ident_f", [128, 128])
        ident_b = sb("ident_b", [128, 128], BF16)
        ones_f = sb("ones_f", [128, 1024])
        ppt = sb("ppt", [128, NPP])
        bvt = sb("bvt", [128, BVN])
        modT = sb("modT", [128, 48, 2])
        lbt = sb("lbt", [128, 3, 8])
        rgc = sb("rgc", [128, 2, 8])
        keysT = sb("keysT", [128, 16, 128], BF16)
        msk = sb("msk", [128, 2, 128])
        iota16 = sb("iota16", [128, 16])
        rgout = sb("rgout", [128, 32])

        A('pool', 'memset', W=['ident_f'], ap=ident_f[:], constant=1.0)
        A('pool', 'affine_select', R=['ident_f'], W=['ident_f'], out=ident_f[:], in_=ident_f[:], pattern=[[-1, 128]],
          compare_op=ALU.is_equal, fill=0.0, base=0, channel_multiplier=1)
        A('pool', 'tensor_copy', R=['ident_f'], W=['ident_b'], out=ident_b[:], in_=ident_f[:])
        A('pool', 'memset', W=['ones_f'], ap=ones_f[:], constant=1.0)
        A('pool', 'memset', W=['rgout'], ap=rgout[:], constant=0.0)
        A('pool', 'memset', W=['msk'], ap=msk[:], constant=1.0)
        A('pool', 'affine_select', R=['msk'], W=['msk'], out=msk[:, 0, :], in_=msk[:, 0, :], pattern=[[1, 128]],
          compare_op=ALU.is_ge, fill=0.0, base=0, channel_multiplier=-1)
        A('pool', 'affine_select', R=['msk'], W=['msk'], out=msk[:, 1, :], in_=msk[:, 1, :], pattern=[[-1, 128]],
          compare_op=ALU.is_ge, fill=0.0, base=0, channel_multiplier=1)
        for i in range(16):
            A('pool', 'memset', W=['iota16'], ap=iota16[:, i:i + 1], constant=float(i))
        DMA('sp', 'dma_start', W=['ppt'], out=ppt[:], in_=pp[:, :])
        DMA('sp', 'dma_start', W=['bvt'], out=bvt[:], in_=bvd.partition_broadcast(128))

        with ExitStack() as ss:
            wbuf = [sb("wbuf%d" % i, [128, 8, 768], F32, ss) for i in range(2)]
            scond = sb("scond", [128, 8, 2], BF16, ss)
            wb16 = [sb("wb16_%d" % i, [128, 8, 768], BF16, ss) for i in range(2)]
            ks = sb("ks", [128, 16, 128], F32, ss)
            tmp8 = sb("tmp8", [128, 8], F32, ss)
            ps_mod = Q[3][:, 0:96].rearrange("p (j c) -> p j c", c=2)

            A('act', 'activation', R=['ppt'], W=['scond'], out=scond[:].rearrange("p k c -> p (k c)"),
              in_=ppt[:, PP_COND:PP_COND + 16], func=AF.Silu)
            for jb in range(8):
                b = jb % 2
                for k in range(8):
                    DMA('sp', 'dma_start', W=['wbuf%d_%d' % (b, k)], out=wbuf[b][:, k, :],
                        in_=w_ada[k * 128:(k + 1) * 128, jb * 768:(jb + 1) * 768])
                    eng = ('act', 'dve', 'pool', 'dve')[k % 4]
                    A(eng, 'copy' if eng == 'act' else 'tensor_copy', R=['wbuf%d_%d' % (b, k)], W=['wb16_%d_%d' % (b, k)],
                      out=wb16[b][:, k, :], in_=wbuf[b][:, k, :])
                for jj in range(6):
                    j = jb * 6 + jj
                    for k in range(8):
                        A('pe', 'matmul', R=['wb16_%d_%d' % (b, k), 'scond'], W=['B30'], inc=(jj == 5 and k == 7),
                          out=ps_mod[:, j, :], lhsT=wb16[b][:, k, jj * 128:(jj + 1) * 128], rhs=scond[:, k, :],
                          start=(k == 0), stop=(k == 7))
            A('dve', 'tensor_tensor', R=['B30', 'ppt'], W=['modT'], out=modT[:], in0=ps_mod,
              in1=ppt[:, PP_BADA:PP_BADA + 48].unsqueeze(2).to_broadcast([128, 48, 2]), op=ALU.add)
            for base in (8, 32):
                A('dve', 'tensor_scalar_add', R=['modT'], W=['modT'], out=modT[:, base:base + 8, :],
                  in0=modT[:, base:base + 8, :], scalar1=1.0)
            if debug:
                DMA('sp', 'dma_start', R=['modT'], out=dbg['modT'][:, :], in_=modT[:].rearrange("p j c -> p (j c)"))
            lbv = ppt[:, PP_LB:PP_LB + 16].rearrange("p (d s h) -> p d s h", d=2, s=2)
            A('dve', 'tensor_tensor', R=['ppt'], W=['tmp8'], out=tmp8[:].rearrange("p (d h) -> p d h", d=2),
              in0=lbv[:, :, 0, :], in1=lbv[:, :, 1, :], op=ALU.subtract)
            A('act', 'activation', R=['tmp8'], W=['lbt0'], out=lbt[:, 0, :], in_=tmp8[:], func=AF.Sigmoid)
            A('dve', 'tensor_scalar', R=['lbt0'], W=['lbt1'], out=lbt[:, 1, :], in0=lbt[:, 0, :], scalar1=-1.0,
              scalar2=1.0, op0=ALU.mult, op1=ALU.add)
            A('dve', 'tensor_scalar_add', R=['lbt0'], W=['lbt2'], out=lbt[:, 2, :], in0=lbt[:, 0, :], scalar1=-1.0)
            A('act', 'activation', R=['ppt'], W=['rgc0'], out=rgc[:, 0, :], in_=ppt[:, PP_LAM:PP_LAM + 8],
              func=AF.Exp, scale=-1.0)
            A('act', 'activation', R=['rgc0'], W=['rgc0'], out=rgc[:, 0, :], in_=rgc[:, 0, :], func=AF.Ln, scale=1.0,
              bias=1.0)
            A('dve', 'tensor_scalar_mul', R=['rgc0'], W=['rgc1'], out=rgc[:, 1, :], in0=rgc[:, 0, :], scalar1=-16.0)
            A('dve', 'tensor_scalar_mul', R=['rgc0', 'rgc1'], W=['rgc0'], out=rgc[:, 0, :], in0=rgc[:, 0, :],
              scalar1=-8.0)
            DMA('sp', 'dma_start', W=['ks'], out=ks[:], in_=keys.rearrange("a k d -> k a d"))
            for hp in range(16):
                A('pe', 'transpose', R=['ks', 'ident_f'], W=['B%d%d' % (hp // 8, (hp % 8) // 4)],
                  out=Q[hp // 8][:, (hp % 8) * 128:(hp % 8 + 1) * 128], in_=ks[:, hp, :], identity=ident_f[:])
            for i in range(2):
                A('act', 'copy', R=['B%d0' % i, 'B%d1' % i], W=['keysT'],
                  out=keysT[:, i * 8:(i + 1) * 8, :].rearrange("p a k -> p (a k)"), in_=Q[i][:])
            P.barrier()


        for g in groups:
            cnd = g
            with ExitStack() as gs_:
                x1s = sb("x1s%d" % g, [128, 8, D], F32, gs_)
                idx_all = sb("idx%d" % g, [128, 8, 128], I32, gs_)
                gate_all = sb("gate%d" % g, [128, 8, 128], F32, gs_)
                bc = [sb("bc%d_%d" % (g, i), [128, D], F32, gs_) for i in range(4)]
                mx_ = ExitStack()
                mixT = sb("mixT%d" % g, [128, 8, 1024], BF16, mx_)
                Fb = [x1s[:, i, :] for i in range(8)]
                with ExitStack() as ms:
                    xt = [Fb[6], Fb[7]]
                    hT = sb("hT%d" % g, [128, 8, 1024], BF16, ms)
                    wst = [sb("wst%d_%d" % (g, i), [128, 640], F32, ms) for i in range(4)]
                    wib = [sb("wib%d_%d" % (g, i), [128, 8, 640], BF16, ms) for i in range(2)]
                    Hb = [sb("H%d_%d" % (g, i), [128, 1024], BF16, ms) for i in range(6)]
                    vt = sb("vt%d" % g, [128, 8, 128], BF16, ms)
                    gsb = sb("gs%d" % g, [128, 8, 128], BF16, ms)
                    kt = sb("kt%d" % g, [128, 2, 8, 128], BF16, ms)
                    Ssc = sb("Ssc%d" % g, [128, 2, 8, 128], BF16, ms)
                    Sst = [[sb("S%d_%d_%d" % (g, d, i), [128, 128], F32, ms) for i in range(2)] for d in range(2)]
                    stmp = sb("stmp%d" % g, [128, 128], F32, ms)
                    att = [sb("att%d_%d" % (g, i), [128, 2, 128], BF16, ms) for i in range(2)]
                    ogt = [sb("og%d_%d" % (g, i), [128, 128], BF16, ms) for i in range(2)]
                    gtmp = [sb("gtmp%d_%d" % (g, i), [128, 128], F32, ms) for i in range(2)]
                    sqj = sb("sqj%d" % g, [128, 128], F32, ms)
                    bnd = sb("bnd%d" % g, [128, 2, 3, 8], F32, ms)
                    bdf = sb("bdf%d" % g, [128, 2, 3, 8], F32, ms)
                    rcm = sb("rcm%d" % g, [128, 2, 8], F32, ms)
                    epv = sb("epv%d" % g, [128, 2, 8], F32, ms)
                    rms = sb("rms%d" % g, [128, 2, 4], F32, ms)
                    wg_st = sb("wgst%d" % g, [128, 4, 128], F32, ms)
                    wgb = sb("wgb%d" % g, [128, 4, 128], BF16, ms)

                    pX = Q[0]
                    pz = [Q[1][:, 0:512], Q[1][:, 512:1024]]
                    pads = Q[2][:].rearrange("p (a b) -> p a b", b=128)
                    po = Q[3][:, 512:1024].rearrange("p (a b) -> p a b", b=128)
                    pbf = Q[3][:, 0:512].bitcast(BF16).rearrange("p (a b) -> p a b", b=128)

                    for t in range(8):
                        b = t % 2
                        DMA('sp', 'dma_start', W=['xt%d' % b], out=xt[b], in_=xg[g][t * 128:(t + 1) * 128, :])
                        for k in range(8):
                            A('pe', 'transpose', R=['xt%d' % b, 'ident_f'], W=['B0%d' % (k // 4)],
                              out=pX[:, k * 128:(k + 1) * 128], in_=xt[b][:, k * 128:(k + 1) * 128],
                              identity=ident_f[:])
                        import os as _os
                        _dbgv = _os.environ.get('K_DBG', '')
                        for k in range(8):
                            sc = modT[:, 8 + k, cnd:cnd + 1]
                            sh = modT[:, 0 + k, cnd:cnd + 1]
                            if _dbgv == 'noevac':
                                continue
                            if _dbgv == 'plain':
                                A('act', 'copy', R=['B0%d' % (k // 4)], W=['hT%d_%d' % (k, t)],
                                  out=hT[:, k, t * 128:(t + 1) * 128], in_=pX[:, k * 128:(k + 1) * 128])
                                continue
                            if (k < 4 or _dbgv == 'act') and _dbgv != 'dve':
                                A('act', 'activation', R=['B0%d' % (k // 4), 'modT'], W=['hT%d_%d' % (k, t)],
                                  out=hT[:, k, t * 128:(t + 1) * 128], in_=pX[:, k * 128:(k + 1) * 128],
                                  func=AF.Identity, scale=sc, bias=sh)
                            else:
                                A('dve', 'tensor_scalar', R=['B0%d' % (k // 4), 'modT'], W=['hT%d_%d' % (k, t)],
                                  out=hT[:, k, t * 128:(t + 1) * 128], in0=pX[:, k * 128:(k + 1) * 128],
                                  scalar1=sc, scalar2=sh, op0=ALU.mult, op1=ALU.add)
                    P.barrier()
                    STOP('hT')

                    wslot = [0]

                    def load_unit_weights(u, ub):
                        nblk = 5 if u < 4 else 2
                        for k in range(8):
                            s = wslot[0] % 4
                            wslot[0] += 1
                            if u < 4:
                                src = w_in[k * 128:(k + 1) * 128, 0:2560].rearrange(
                                    "p (i h c) -> p i h c", h=4, c=128)[:, :, u, :]
                            else:
                                src = w_in[k * 128:(k + 1) * 128, 2560:3584].rearrange(
                                    "p (i h c) -> p i h c", h=4, c=128)[:, :, u - 4, :]
                            DMA('sp', 'dma_start', W=['wst%d' % s],
                                out=wst[s][:, 0:nblk * 128].rearrange("p (i c) -> p i c", c=128), in_=src)
                            A('pool', 'tensor_copy', R=['wst%d' % s], W=['wib%d_%d' % (ub, k)],
                              out=wib[ub][:, k, 0:nblk * 128], in_=wst[s][:, 0:nblk * 128])

                    zc = [0]

                    def zproj_feat(ub, blk, hf):
                        pi = zc[0] % 2
                        zc[0] += 1
                        for k in range(8):
                            A('pe', 'matmul', R=['wib%d_%d' % (ub, k)], W=['B1%d' % pi], inc=(k == 7),
                              out=pz[pi], lhsT=wib[ub][:, k, blk * 128:(blk + 1) * 128],
                              rhs=hT[:, k, hf * 512:(hf + 1) * 512], start=(k == 0), stop=(k == 7))
                        return pz[pi], 'B1%d' % pi

                    qs, kk, qd = Hb[0], [Hb[1], Hb[2]], [Hb[3], Hb[4]]
                    sg, Eb, Tx = [Fb[0], Fb[1]], [Fb[2], Fb[3]], [Fb[4], Fb[5]]
                    nseq = 4 if g == 0 else 1
                    tps = 8 // nseq
                    load_unit_weights(0, 0)
                    for h in range(4):
                        ub = h % 2
                        for hf in range(2):
                            ps_, tk = zproj_feat(ub, 0, hf)
                            A('act', 'activation', R=[tk], W=['qs'], out=qs[:, hf * 512:(hf + 1) * 512], in_=ps_,
                              func=AF.Silu)
                        for d in range(2):
                            for hf in range(2):
                                ps_, tk = zproj_feat(ub, 2 + d, hf)
                                A('act', 'activation', R=[tk], W=['sg%d' % d], out=sg[d][:, hf * 512:(hf + 1) * 512],
                                  in_=ps_, func=AF.Sigmoid)
                        for t in range(8):
                            pi = zc[0] % 2
                            zc[0] += 1
                            for bi, blk in enumerate((1, 4)):
                                for k in range(8):
                                    A('pe', 'matmul', R=['wib%d_%d' % (ub, k)], W=['B1%d' % pi],
                                      inc=(k == 7 and bi == 1),
                                      out=pz[pi][:, bi * 128:(bi + 1) * 128], lhsT=hT[:, k, t * 128:(t + 1) * 128],
                                      rhs=wib[ub][:, k, blk * 128:(blk + 1) * 128], start=(k == 0), stop=(k == 7))
                            A('act', 'copy', R=['B1%d' % pi], W=['vt%d' % t], out=vt[:, t, :], in_=pz[pi][:, 0:128])
                            gi = t % 2
                            A('act', 'activation', R=['B1%d' % pi], W=['gtmp%d' % gi], out=gtmp[gi][:],
                              in_=pz[pi][:, 128:256], func=AF.Silu)
                            A('dve', 'tensor_tensor', R=['gtmp%d' % gi, 'bvt'], W=['gs%d' % t], out=gsb[:, t, :],
                              in0=gtmp[gi][:], in1=bvt[:, BV_NG + h * 128:BV_NG + (h + 1) * 128], op=ALU.mult)

                        load_unit_weights(h + 1, (h + 1) % 2)
                        STOP('zproj')
                        for d in range(2):
                            lb_c = lbt[:, 0, d * 4 + h:d * 4 + h + 1]
                            oml_c = lbt[:, 1, d * 4 + h:d * 4 + h + 1]
                            noml_c = lbt[:, 2, d * 4 + h:d * 4 + h + 1]
                            A('dve', 'tensor_scalar', R=['sg%d' % d, 'lbt1', 'lbt2'], W=['kk%d' % d], out=kk[d][:],
                              in0=sg[d], scalar1=noml_c, scalar2=oml_c, op0=ALU.mult, op1=ALU.add)
                            A('act', 'activation', R=['sg%d' % d, 'lbt0', 'lbt1'], W=['sg%d' % d], out=sg[d], in_=sg[d],
                              func=AF.Ln, scale=oml_c, bias=lb_c)
                            A('dve', 'tensor_tensor_scan', R=['ones_f', 'sg%d' % d], W=['Eb%d' % d], out=Eb[d],
                              data0=ones_f[:], data1=sg[d], initial=0.0, op0=ALU.mult, op1=ALU.add)
                            V = Eb[d].rearrange("p (t c) -> p t c", c=128)
                            LV = sg[d].rearrange("p (t c) -> p t c", c=128)
                            if d == 0:
                                A('dve', 'memset', W=['epv0'], ap=epv[:, 0, 0:1], constant=0.0)
                                A('dve', 'tensor_copy', R=['Eb0', 'epv0'], W=['epv0'], out=epv[:, 0, 1:8],
                                  in_=V[:, 0:7, 127])
                                A('dve', 'tensor_copy', R=['Eb0'], W=['rcm0'], out=rcm[:, 0, :], in_=V[:, :, 63])
                                A('dve', 'tensor_tensor', R=['rcm0', 'epv0'], W=['bdf0'], out=bdf[:, 0, 0, :],
                                  in0=rcm[:, 0, :], in1=epv[:, 0, :], op=ALU.subtract)
                                A('dve', 'tensor_tensor', R=['Eb0', 'epv0', 'bdf0'], W=['bdf0'], out=bdf[:, 0, 1, :],
                                  in0=V[:, :, 127], in1=epv[:, 0, :], op=ALU.subtract)
                                A('dve', 'tensor_tensor', R=['Eb0', 'rcm0', 'bdf0'], W=['bdf0'], out=bdf[:, 0, 2, :],
                                  in0=V[:, :, 127], in1=rcm[:, 0, :], op=ALU.subtract)
                            else:
                                A('dve', 'tensor_tensor', R=['Eb1', 'sg1'], W=['Eb1'], out=Eb[1], in0=Eb[1], in1=sg[1],
                                  op=ALU.subtract)
                                A('dve', 'tensor_copy', R=['Eb1'], W=['epv1'], out=epv[:, 1, 0:7], in_=V[:, 1:8, 0])
                                A('dve', 'tensor_tensor', R=['Eb1', 'sg1', 'epv1'], W=['epv1'], out=epv[:, 1, 7:8],
                                  in0=V[:, 7, 127:128], in1=LV[:, 7, 127:128], op=ALU.add)
                                A('dve', 'tensor_copy', R=['Eb1'], W=['rcm1'], out=rcm[:, 1, :], in_=V[:, :, 64])
                                A('dve', 'tensor_tensor', R=['rcm1', 'epv1'], W=['bdf1'], out=bdf[:, 1, 0, :],
                                  in0=epv[:, 1, :], in1=rcm[:, 1, :], op=ALU.subtract)
                                A('dve', 'tensor_tensor', R=['Eb1', 'epv1', 'bdf1'], W=['bdf1'], out=bdf[:, 1, 1, :],
                                  in0=epv[:, 1, :], in1=V[:, :, 0], op=ALU.subtract)
                                A('dve', 'tensor_tensor', R=['Eb1', 'rcm1', 'bdf1'], W=['bdf1'], out=bdf[:, 1, 2, :],
                                  in0=rcm[:, 1, :], in1=V[:, :, 0], op=ALU.subtract)
                            A('act', 'activation', R=['bdf%d' % d], W=['bnd%d' % d],
                              out=bnd[:, d, :, :].rearrange("p a t -> p (a t)"),
                              in_=bdf[:, d, :, :].rearrange("p a t -> p (a t)"), func=AF.Exp)
                            A('dve', 'tensor_tensor', R=['Eb%d' % d, 'rcm%d' % d], W=['Eb%d' % d], out=V, in0=V,
                              in1=rcm[:, d, :].unsqueeze(2).to_broadcast([128, 8, 128]), op=ALU.subtract)
                            sq_, sk_ = (1.0, -1.0) if d == 0 else (-1.0, 1.0)
                            A('act', 'activation', R=['Eb%d' % d], W=['Tx0'], out=Tx[0], in_=Eb[d], func=AF.Exp,
                              scale=sq_)
                            A('dve', 'tensor_tensor', R=['qs', 'Tx0'], W=['qd%d' % d], out=qd[d][:], in0=qs[:],
                              in1=Tx[0], op=ALU.mult)
                            A('act', 'activation', R=['Eb%d' % d], W=['Tx1'], out=Tx[1], in_=Eb[d], func=AF.Exp,
                              scale=sk_)
                            A('dve', 'tensor_tensor', R=['kk%d' % d, 'Tx1'], W=['kk%d' % d], out=kk[d][:],
                              in0=kk[d][:], in1=Tx[1], op=ALU.mult)

                        STOP('prep')
                        zero_in = {}
                        for d in range(2):
                            order = list(range(8)) if d == 0 else list(range(7, -1, -1))
                            cur = 0
                            s_zero = True
                            for oi, t in enumerate(order):
                                seq = t // tps
                                first = (t % tps == 0) if d == 0 else (t % tps == tps - 1)
                                last = (t % tps == tps - 1) if d == 0 else (t % tps == 0)
                                sl = oi % 2
                                A('pe', 'transpose', R=['kk%d' % d, 'ident_b'], W=['B30'],
                                  out=pbf[:, d * 2 + sl, :], in_=kk[d][:, t * 128:(t + 1) * 128], identity=ident_b[:])
                                A('act', 'copy', R=['B30'], W=['kt%d_%d' % (d, t)],
                                  out=kt[:, d, t, :], in_=pbf[:, d * 2 + sl, :])
                                A('pe', 'matmul', R=['kt%d_%d' % (d, t), 'vt%d' % t], W=['B21'],
                                  out=pads[:, 4 + d, :], lhsT=kt[:, d, t, :], rhs=vt[:, t, :], start=True, stop=True)
                                if first:
                                    if g == 0:
                                        s_zero = True
                                    else:
                                        s_zero = False
                                        DMA('sp', 'dma_start', W=['S%d_%d' % (d, cur)], out=Sst[d][cur][:],
                                            in_=st_hg[d, h, :, :])
                                zero_in[(d, t)] = s_zero
                                nxt = 1 - cur
                                wk_c = bnd[:, d, 2, t:t + 1]
                                wdec_c = bnd[:, d, 1, t:t + 1]
                                win_c = bnd[:, d, 0, t:t + 1]
                                if s_zero:
                                    A('dve', 'tensor_scalar', R=['B21', 'bnd%d' % d], W=['S%d_%d' % (d, nxt)],
                                      out=Sst[d][nxt][:], in0=pads[:, 4 + d, :], scalar1=wk_c, scalar2=None,
                                      op0=ALU.mult)
                                else:
                                    A('act', 'activation', R=['S%d_%d' % (d, cur), 'bnd%d' % d],
                                      W=['Ssc%d_%d' % (d, t)], out=Ssc[:, d, t, :], in_=Sst[d][cur][:], func=AF.Copy,
                                      scale=win_c)
                                    A('dve', 'tensor_scalar', R=['B21', 'bnd%d' % d], W=['stmp'], out=stmp[:],
                                      in0=pads[:, 4 + d, :], scalar1=wk_c, scalar2=None, op0=ALU.mult)
                                    A('dve', 'scalar_tensor_tensor', R=['S%d_%d' % (d, cur), 'stmp', 'bnd%d' % d],
                                      W=['S%d_%d' % (d, nxt)], out=Sst[d][nxt][:], in0=Sst[d][cur][:], scalar=wdec_c,
                                      in1=stmp[:], op0=ALU.mult, op1=ALU.add)
                                s_zero = False
                                cur = nxt
                                if last and g == 0:
                                    DMA('sp', 'dma_start', R=['S%d_%d' % (d, cur)], out=nhg[seq, d, h, :, :],
                                        in_=Sst[d][cur][:])

                        STOP('pass1')
                        for t in range(8):
                            sl = t % 2
                            for d in range(2):
                                A('pe', 'matmul', R=['kk%d' % d, 'qd%d' % d], W=['B20'], out=pads[:, d, :],
                                  lhsT=kk[d][:, t * 128:(t + 1) * 128], rhs=qd[d][:, t * 128:(t + 1) * 128],
                                  start=True, stop=True)
                            A('dve', 'tensor_tensor', R=['B20', 'msk'], W=['att%d' % sl], out=att[sl][:],
                              in0=pads[:, 0:2, :], in1=msk[:], op=ALU.mult)
                            mm = [(att[sl][:, 0, :], vt[:, t, :], ['att%d' % sl, 'vt%d' % t]),
                                  (att[sl][:, 1, :], vt[:, t, :], ['att%d' % sl, 'vt%d' % t])]
                            for d in range(2):
                                if not zero_in[(d, t)]:
                                    mm.append((qd[d][:, t * 128:(t + 1) * 128], Ssc[:, d, t, :],
                                               ['qd%d' % d, 'Ssc%d_%d' % (d, t)]))
                            for i, (l_, r_, rd) in enumerate(mm):
                                A('pe', 'matmul', R=rd, W=['B31'], inc=(i == len(mm) - 1), out=po[:, sl, :],
                                  lhsT=l_, rhs=r_, start=(i == 0), stop=(i == len(mm) - 1))
                            A('act', 'activation', R=['B31'], W=['sqj', 'rms%d' % sl], out=sqj[:],
                              in_=po[:, sl, :], func=AF.Square, accum_out=rms[:, sl, 0:1])
                            A('act', 'activation', R=['rms%d' % sl], W=['rms%d' % sl], out=rms[:, sl, 1:2],
                              in_=rms[:, sl, 0:1], func=AF.Sqrt, scale=1.0 / 128.0, bias=RMS_EPS)
                            A('dve', 'reciprocal', R=['rms%d' % sl], W=['rms%d' % sl], out=rms[:, sl, 2:3],
                              in_=rms[:, sl, 1:2])
                            A('dve', 'scalar_tensor_tensor', R=['B31', 'rms%d' % sl, 'gs%d' % t],
                              W=['og%d' % sl], out=ogt[sl][:], in0=po[:, sl, :], scalar=rms[:, sl, 2:3],
                              in1=gsb[:, t, :], op0=ALU.mult, op1=ALU.mult)
                            A('pe', 'transpose', R=['og%d' % sl, 'ident_b'], W=['B30'],
                              out=pbf[:, 4 + sl, :], in_=ogt[sl][:], identity=ident_b[:])
                            A('act', 'copy', R=['B30'], W=['mixT%d' % h],
                              out=mixT[:, h, t * 128:(t + 1) * 128], in_=pbf[:, 4 + sl, :])
                    P.barrier()
                    STOP('hg')

                    xr, gg, xc = Fb[0], Fb[1], Fb[2]
                    ra, ia, a2 = Fb[3], Fb[4], Fb[5]
                    xcb = Hb[0]
                    hh = [Fb[6], Fb[7]]
                    if g == 0:
                        L = 256
                    else:
                        L = 16

                    def perm_out(buf, hf):
                        if g == 0:
                            return buf[:, hf * 512:(hf + 1) * 512]
                        return buf.rearrange("p (c r) -> p r c", r=16)[:, hf * 8:(hf + 1) * 8, :]

                    def perm_in(ps_):
                        if g == 0:
                            return ps_
                        return ps_.rearrange("p (r c) -> p r c", c=64)

                    for c in range(4):
                        ub = c % 2
                        for hf in range(2):
                            ps_, tk = zproj_feat(ub, 0, hf)
                            A('act', 'activation', R=[tk], W=['xr'], out=perm_out(xr, hf), in_=perm_in(ps_),
                              func=AF.Copy)
                        for hf in range(2):
                            ps_, tk = zproj_feat(ub, 1, hf)
                            A('act', 'activation', R=[tk], W=['gg'], out=perm_out(gg, hf), in_=perm_in(ps_),
                              func=AF.Gelu_apprx_tanh)
                        if c < 3:
                            load_unit_weights(4 + c + 1, (c + 1) % 2)
                        A('pool', 'memset', W=['wg_st'], ap=wg_st[:], constant=0.0)
                        for gate in range(2):
                            for d in range(2):
                                for blk in range(2):
                                    DMA('sp', 'dma_start', W=['wg_st'],
                                        out=wg_st[blk * 64:(blk + 1) * 64, gate * 2 + d, blk * 64:(blk + 1) * 64],
                                        in_=rgw[gate, d, 2 * c + blk, :, :])
                        A('pool', 'tensor_copy', R=['wg_st'], W=['wgb'], out=wgb[:], in_=wg_st[:])
                        xv = xr.rearrange("p (s l) -> p s l", l=L)
                        cv = xc.rearrange("p (s l) -> p s l", l=L)

                        def cw(tap):
                            return ppt[:, PP_CW + tap * 4 + c:PP_CW + tap * 4 + c + 1]
                        A('dve', 'tensor_scalar', R=['xr', 'ppt'], W=['xc'], out=xc, in0=xr, scalar1=cw(2),
                          scalar2=ppt[:, PP_CB + c:PP_CB + c + 1], op0=ALU.mult, op1=ALU.add)
                        A('dve', 'scalar_tensor_tensor', R=['xr', 'ppt', 'xc'], W=['xc'], out=cv[:, :, 2:L],
                          in0=xv[:, :, 0:L - 2], scalar=cw(0), in1=cv[:, :, 2:L], op0=ALU.mult, op1=ALU.add)
                        A('dve', 'scalar_tensor_tensor', R=['xr', 'ppt', 'xc'], W=['xc'], out=cv[:, :, 1:L],
                          in0=xv[:, :, 0:L - 1], scalar=cw(1), in1=cv[:, :, 1:L], op0=ALU.mult, op1=ALU.add)
                        A('dve', 'scalar_tensor_tensor', R=['xr', 'ppt', 'xc'], W=['xc'], out=cv[:, :, 0:L - 1],
                          in0=xv[:, :, 1:L], scalar=cw(3), in1=cv[:, :, 0:L - 1], op0=ALU.mult, op1=ALU.add)
                        A('act', 'copy', R=['xc'], W=['xcb'], out=xcb[:], in_=xc)
                        for d in range(2):
                            for gate, dst, bcol in ((0, ra, PP_BR), (1, ia, PP_BI)):
                                for hf in range(2):
                                    pi = zc[0] % 2
                                    zc[0] += 1
                                    A('pe', 'matmul', R=['wgb', 'xcb'], W=['B1%d' % pi], out=pz[pi],
                                      lhsT=wgb[:, gate * 2 + d, :], rhs=xcb[:, hf * 512:(hf + 1) * 512],
                                      start=True, stop=True)
                                    A('act', 'activation', R=['B1%d' % pi, 'ppt'], W=['rg_%d' % gate],
                                      out=dst[:, hf * 512:(hf + 1) * 512], in_=pz[pi], func=AF.Sigmoid,
                                      bias=ppt[:, bcol + d * 4 + c:bcol + d * 4 + c + 1])
                            ca_c = rgc[:, 0, d * 4 + c:d * 4 + c + 1]
                            ca2_c = rgc[:, 1, d * 4 + c:d * 4 + c + 1]
                            A('act', 'activation', R=['rg_0', 'rgc1'], W=['a2'], out=a2, in_=ra, func=AF.Exp, scale=ca2_c)
                            A('act', 'activation', R=['rg_0', 'rgc0'], W=['rg_0'], out=ra, in_=ra, func=AF.Exp,
                              scale=ca_c)
                            A('act', 'activation', R=['a2'], W=['a2'], out=a2, in_=a2, func=AF.Sqrt, scale=-1.0, bias=1.0)
                            A('dve', 'tensor_tensor', R=['a2', 'rg_1'], W=['a2'], out=a2, in0=a2, in1=ia, op=ALU.mult)
                            A('dve', 'tensor_tensor', R=['a2', 'xc'], W=['a2'], out=a2, in0=a2, in1=xc, op=ALU.mult)
                            nscan = 4 if g == 0 else 1
                            Ls = 1024 // nscan
                            for s in range(nscan):
                                lo, hi = s * Ls, (s + 1) * Ls
                                if g == 0:
                                    init = 0.0
                                else:
                                    init = ppt[:, PP_SRG + d * 4 + c:PP_SRG + d * 4 + c + 1]
                                if d == 0:
                                    o_, a_, u_ = hh[0][:, lo:hi], ra[:, lo:hi], a2[:, lo:hi]
                                else:
                                    o_, a_, u_ = hh[1][:, lo:hi][:, ::-1], ra[:, lo:hi][:, ::-1], a2[:, lo:hi][:, ::-1]
                                A('dve', 'tensor_tensor_scan', R=['rg_0', 'a2', 'ppt'], W=['hh%d' % d], out=o_,
                                  data0=a_, data1=u_, initial=init, op0=ALU.mult, op1=ALU.add)
                                if g == 0:
                                    col = s * 8 + d * 4 + c
                                    src = hh[0][:, hi - 1:hi] if d == 0 else hh[1][:, lo:lo + 1]
                                    A('act', 'copy', R=['hh%d' % d], W=['rgout'], out=rgout[:, col:col + 1], in_=src)
                        A('dve', 'tensor_tensor', R=['hh0', 'hh1'], W=['hh0'], out=hh[0], in0=hh[0], in1=hh[1],
                          op=ALU.add)
                        if g == 0:
                            y_out = mixT[:, 4 + c, :]
                            y_h, y_g = hh[0], gg
                        else:
                            y_out = mixT[:, 4 + c, :].rearrange("p (r c) -> p c r", c=64)
                            y_h = hh[0].rearrange("p (c r) -> p c r", r=16)
                            y_g = gg.rearrange("p (c r) -> p c r", r=16)
                        A('dve', 'tensor_tensor', R=['hh0', 'gg'], W=['mixT%d' % (4 + c)], out=y_out, in0=y_h, in1=y_g,
                          op=ALU.mult)
                    STOP('rg')
                    if debug:
                        DMA('sp', 'dma_start', R=['mixT%d' % k for k in range(8)], out=dbg['mixT'][g, :, :],
                            in_=mixT[:].rearrange("p k t -> p (k t)"))
                    P.barrier()

                with ExitStack() as rs_:
                    with ExitStack() as p1s:
                        wo_b = sb("wo_b%d" % g, [128, 8, D], BF16, p1s)
                        wst2 = [sb("wst2_%d_%d" % (g, i), [128, 1024], F32, p1s) for i in range(2)]
                        dgb = [sb("dgb%d_%d" % (g, i), [128, 128], F32, p1s) for i in range(2)]
                        xres = [sb("xres%d_%d" % (g, i), [128, D], F32, p1s) for i in range(2)]
                        y1 = sb("y1_%d" % g, [128, D], F32, p1s)
                        stats = sb("stats%d" % g, [128, 2, 6], F32, p1s)
                        mv = sb("mv%d" % g, [128, 4], F32, p1s)
                        for k in range(8):
                            s = k % 2
                            DMA('sp', 'dma_start', W=['wst2_%d' % s], out=wst2[s][:], in_=w_out[k * 128:(k + 1) * 128, :])
                            if do_peer and g == groups[0]:
                                for ci in range(min(8 * k, 32), min(8 * k + 8, 32)):
                                    c, which = ci // 2, ci % 2
                                    src, co = ((pu, 0), (pv, D))[which]
                                    P.swdma('pool', 'dma_start', 'conv%d' % ci,
                                            out=uv16[c * 1024:(c + 1) * 1024, co:co + D], in_=src[c * 1024:(c + 1) * 1024, :])
                            A('act', 'copy', R=['wst2_%d' % s], W=['wo_b%d' % k], out=wo_b[:, k, :], in_=wst2[s][:])
                        for bi, jb in enumerate((16, 32, 24, 40)):
                            for k in range(8):
                                dsl = k % 2
                                A('dve', 'tensor_scalar', R=['ident_f', 'modT'], W=['dgb%d' % dsl], out=dgb[dsl][:],
                                  in0=ident_f[:], scalar1=modT[:, jb + k, cnd:cnd + 1], scalar2=None, op0=ALU.mult)
                                A('pe', 'matmul', R=['ones_f', 'dgb%d' % dsl], W=['B0%d' % (k // 4)],
                                  out=Q[0][:, k * 128:(k + 1) * 128], lhsT=ones_f[:, 0:128], rhs=dgb[dsl][:],
                                  start=True, stop=True)
                            A('act', 'copy', R=['B00', 'B01'], W=['bc%d' % bi], out=bc[bi][:], in_=Q[0][:])
                        for t in range(8):
                            xb = t % 2
                            DMA('sp', 'dma_start', W=['xres%d' % xb], out=xres[xb][:],
                                in_=xg[g][t * 128:(t + 1) * 128, :])
                            for hf in range(2):
                                for k in range(8):
                                    A('pe', 'matmul', R=['wo_b%d' % k], W=['B0%d' % hf], inc=(k == 7),
                                      out=Q[0][:, hf * 512:(hf + 1) * 512], lhsT=mixT[:, k, t * 128:(t + 1) * 128],
                                      rhs=wo_b[:, k, hf * 512:(hf + 1) * 512], start=(k == 0), stop=(k == 7))
                            A('dve', 'tensor_tensor', R=['B00', 'B01', 'bc0'], W=['y1'], out=y1[:], in0=Q[0][:], in1=bc[0][:],
                              op=ALU.mult)
                            A('dve', 'scalar_tensor_tensor', R=['xres%d' % xb, 'y1'], W=['y1'], out=y1[:],
                              in0=xres[xb][:], scalar=ALPHA, in1=y1[:], op0=ALU.mult, op1=ALU.add)
                            ln_block(A, y1, x1s[:, t, :], stats, mv, bvt, BV_L1G, BV_L1B, 'y1', 'x1s%d' % t)
                        STOP('p1a')
                        if debug:
                            DMA('sp', 'dma_start', R=['x1s%d' % t for t in range(8)], out=dbg['x1'][g, :, :],
                                in_=x1s[:].rearrange("p t d -> p (t d)"))
                        P.barrier()
                    mx_.close()

                    with ExitStack() as p1s:
                        wq_b = sb("wq_b%d" % g, [128, 8, 2048], BF16, p1s)
                        wst2 = [sb("wst3_%d_%d" % (g, i), [128, 512], F32, p1s) for i in range(2)]
                        h2_ = [sb("h2_%d_%d" % (g, i), [128, D], F32, p1s) for i in range(2)]
                        h2T_ = [sb("h2T%d_%d" % (g, i), [128, 8, 128], BF16, p1s) for i in range(2)]
                        qTb_ = [sb("qTb%d_%d" % (g, i), [128, 16, 128], BF16, p1s) for i in range(2)]
                        iota_ps = Q[3][:, 0:16]
                        sif_ps = Q[3][:, 16:272].rearrange("p (a k) -> p a k", k=16)
                        scs = sb("scs%d" % g, [128, 16, 128], F32, p1s)
                        scw = [sb("scw%d_%d" % (g, i), [128, 128], F32, p1s) for i in range(16)]
                        sv = sb("sv%d" % g, [128, 16, 16], F32, p1s)
                        si = sb("si%d" % g, [128, 16, 16], U32, p1s)
                        cand = sb("cand%d" % g, [128, 8, 256], F32, p1s)
                        cwk = [sb("cwk%d_%d" % (g, i), [128, 256], F32, p1s) for i in range(8)]
                        fv = sb("fv%d" % g, [128, 8, 16], F32, p1s)
                        fi = sb("fi%d" % g, [128, 8, 16], U32, p1s)
                        fab = sb("fab%d" % g, [128, 2, 128], U32, p1s)
                        fabf = sb("fabf%d" % g, [128, 2, 128], F32, p1s)
                        oh_ = [sb("oh%d_%d" % (g, i), [128, 8, 256], BF16, p1s) for i in range(2)]
                        isel = sb("isel%d" % g, [128, 2, 128], F32, p1s)
                        idxf = sb("idxf%d" % g, [128, 128], F32, p1s)
                        gex = sb("gex%d" % g, [128, 8, 16], F32, p1s)
                        gsm = sb("gsm%d" % g, [128, 2, 8], F32, p1s)
                        ws = 0
                        for k in range(8):
                            for cb in range(4):
                                s = ws % 2
                                ws += 1
                                DMA('sp', 'dma_start', W=['wst3_%d' % s], out=wst2[s][:],
                                    in_=wq[k * 128:(k + 1) * 128, cb * 512:(cb + 1) * 512])
                                if s == 0:
                                    A('pool', 'tensor_copy', R=['wst3_%d' % s], W=['wq_b%d' % k],
                                      out=wq_b[:, k, cb * 512:(cb + 1) * 512], in_=wst2[s][:])
                                else:
                                    A('act', 'copy', R=['wst3_%d' % s], W=['wq_b%d' % k],
                                      out=wq_b[:, k, cb * 512:(cb + 1) * 512], in_=wst2[s][:])
                        A('dve', 'tensor_copy', R=['iota16'], W=['B30'], out=iota_ps, in_=iota16[:])
                        def p1b_front(t):
                            tb = t % 2
                            h2, h2T, qTb = h2_[tb], h2T_[tb], qTb_[tb]
                            A('pool', 'tensor_tensor', R=['bc1'], W=['h2_%d' % tb], out=h2[:], in0=x1s[:, t, :], in1=bc[1][:],
                              op=ALU.mult)
                            A('pool', 'tensor_tensor', R=['h2_%d' % tb, 'bc2'], W=['h2_%d' % tb], out=h2[:], in0=h2[:],
                              in1=bc[2][:], op=ALU.add)
                            for k in range(8):
                                A('pe', 'transpose', R=['h2_%d' % tb, 'ident_f'], W=['B0%d' % (k // 4)],
                                  out=Q[0][:, k * 128:(k + 1) * 128], in_=h2[:, k * 128:(k + 1) * 128], identity=ident_f[:])
                            A('act', 'copy', R=['B00', 'B01'], W=['h2T%d' % tb], out=h2T[:].rearrange("p k t -> p (k t)"),
                              in_=Q[0][:])
                            for hp in range(16):
                                qq = Q[1 + hp // 8]
                                for k in range(8):
                                    A('pe', 'matmul', R=['wq_b%d' % k, 'h2T%d' % tb], W=['B%d%d' % (1 + hp // 8, (hp % 8) // 4)],
                                      inc=(k == 7 and hp % 4 == 3), out=qq[:, (hp % 8) * 128:(hp % 8 + 1) * 128],
                                      lhsT=wq_b[:, k, hp * 128:(hp + 1) * 128], rhs=h2T[:, k, :],
                                      start=(k == 0), stop=(k == 7))
                            A('act', 'copy', R=['B10', 'B11'], W=['qTb%d_0' % tb],
                              out=qTb[:, 0:8, :].rearrange("p a t -> p (a t)"), in_=Q[1][:])
                            A('act', 'copy', R=['B20', 'B21'], W=['qTb%d_1' % tb],
                              out=qTb[:, 8:16, :].rearrange("p a t -> p (a t)"), in_=Q[2][:])
                            for hp in range(16):
                                qq = Q[1 + hp // 8]
                                A('pe', 'matmul', R=['qTb%d_%d' % (tb, hp // 8), 'keysT'],
                                  W=['B%d%d' % (1 + hp // 8, (hp % 8) // 4)],
                                  inc=(hp % 4 == 3), out=qq[:, (hp % 8) * 128:(hp % 8 + 1) * 128], lhsT=qTb[:, hp, :],
                                  rhs=keysT[:, hp, :], start=True, stop=True)
                        def p1b_mid(t):
                            A('act', 'copy', R=['B10', 'B11'], W=['scs0'], out=scs[:, 0:8, :].rearrange("p a t -> p (a t)"),
                              in_=Q[1][:])
                            A('act', 'copy', R=['B20', 'B21'], W=['scs1'], out=scs[:, 8:16, :].rearrange("p a t -> p (a t)"),
                              in_=Q[2][:])
                            for h0 in range(0, 16, 16):
                                hps = range(h0, h0 + 16)
                                for hp in hps:
                                    A('dve', 'max', R=['scs%d' % (hp // 8)], W=['sv%d' % hp], out=sv[:, hp, 0:8],
                                      in_=scs[:, hp, :])
                                for hp in hps:
                                    A('dve', 'max_index', R=['scs%d' % (hp // 8), 'sv%d' % hp], W=['si%d' % hp],
                                      out=si[:, hp, 0:8], in_max=sv[:, hp, 0:8], in_values=scs[:, hp, :])
                                for hp in hps:
                                    A('dve', 'match_replace', R=['scs%d' % (hp // 8), 'sv%d' % hp], W=['scw%d' % (hp % 16)],
                                      out=scw[hp % 16][:], in_to_replace=sv[:, hp, 0:8], in_values=scs[:, hp, :],
                                      imm_value=-1e30)
                                for hp in hps:
                                    A('dve', 'max', R=['scw%d' % (hp % 16)], W=['svb%d' % hp], out=sv[:, hp, 8:16],
                                      in_=scw[hp % 16][:])
                                for hp in hps:
                                    A('dve', 'max_index', R=['scw%d' % (hp % 16), 'svb%d' % hp], W=['sib%d' % hp],
                                      out=si[:, hp, 8:16], in_max=sv[:, hp, 8:16], in_values=scw[hp % 16][:])
                        def p1b_back(t):
                            svt = ['sv%d' % hp for hp in range(16)] + ['svb%d' % hp for hp in range(16)]
                            svv = sv[:].rearrange("p (h q) k -> p h q k", q=2)
                            A('dve', 'tensor_tensor', R=svt, W=['cand'],
                              out=cand[:].rearrange("p h (a b) -> p h a b", b=16),
                              in0=svv[:, :, 0, :].unsqueeze(3).to_broadcast([128, 8, 16, 16]),
                              in1=svv[:, :, 1, :].unsqueeze(2).to_broadcast([128, 8, 16, 16]), op=ALU.add)
                            for h0 in range(0, 8, 8):
                                hs_ = range(h0, h0 + 8)
                                for h_ in hs_:
                                    A('dve', 'max', R=['cand'], W=['fv%d' % h_], out=fv[:, h_, 0:8], in_=cand[:, h_, :])
                                for h_ in hs_:
                                    A('dve', 'max_index', R=['cand', 'fv%d' % h_], W=['fi%d' % h_], out=fi[:, h_, 0:8],
                                      in_max=fv[:, h_, 0:8], in_values=cand[:, h_, :])
                                for h_ in hs_:
                                    A('dve', 'match_replace', R=['cand', 'fv%d' % h_], W=['cwk%d' % (h_ % 8)],
                                      out=cwk[h_ % 8][:], in_to_replace=fv[:, h_, 0:8], in_values=cand[:, h_, :],
                                      imm_value=-1e30)
                                for h_ in hs_:
                                    A('dve', 'max', R=['cwk%d' % (h_ % 8)], W=['fvb%d' % h_], out=fv[:, h_, 8:16],
                                      in_=cwk[h_ % 8][:])
                                for h_ in hs_:
                                    A('dve', 'max_index', R=['cwk%d' % (h_ % 8), 'fvb%d' % h_], W=['fib%d' % h_],
                                      out=fi[:, h_, 8:16], in_max=fv[:, h_, 8:16], in_values=cwk[h_ % 8][:])
                            fvt = ['fv%d' % h_ for h_ in range(8)] + ['fvb%d' % h_ for h_ in range(8)]
                            fif = fi[:].rearrange("p h k -> p (h k)")
                            A('dve', 'tensor_single_scalar', R=['fi%d' % h_ for h_ in range(8)] + ['fib%d' % h_ for h_ in range(8)], W=['fab0'], out=fab[:, 0, :], in_=fif, scalar=4,
                              op=ALU.logical_shift_right)
                            A('dve', 'tensor_single_scalar', R=['fi%d' % h_ for h_ in range(8)] + ['fib%d' % h_ for h_ in range(8)], W=['fab1'], out=fab[:, 1, :], in_=fif, scalar=15,
                              op=ALU.bitwise_and)
                            A('dve', 'tensor_copy', R=['fab0', 'fab1'], W=['fabf'], out=fabf[:], in_=fab[:])
                            A('dve', 'tensor_copy', R=['si%d' % hp for hp in range(16)] + ['sib%d' % hp for hp in range(16)], W=['B30'], out=sif_ps, in_=si[:])
                            sifv = sif_ps.rearrange("p (h q) k -> p h q k", q=2)
                            ohv = [o_[:].rearrange("p h (k a) -> p h k a", a=16) for o_ in oh_]
                            for q_ in range(2):
                                A('dve', 'tensor_tensor', R=['fabf', 'B30'], W=['oh%d' % q_], out=ohv[q_],
                                  in0=fabf[:, q_, :].rearrange("p (h k) -> p h k", k=16).unsqueeze(3).to_broadcast(
                                      [128, 8, 16, 16]),
                                  in1=iota_ps.unsqueeze(1).unsqueeze(1).to_broadcast([128, 8, 16, 16]),
                                  op=ALU.is_equal)
                            for q_ in range(2):
                                A('dve', 'tensor_tensor', R=['oh%d' % q_, 'B30'], W=['oh%d' % q_], out=ohv[q_], in0=ohv[q_],
                                  in1=sifv[:, :, q_, :].unsqueeze(2).to_broadcast([128, 8, 16, 16]), op=ALU.mult)
                            for q_ in range(2):
                                A('dve', 'tensor_reduce', R=['oh%d' % q_], W=['isel%d' % q_], out=isel[:, q_, :],
                                  in_=oh_[q_][:].rearrange("p h (k a) -> p (h k) a", a=16), axis=AX.X, op=ALU.add)
                            A('dve', 'scalar_tensor_tensor', R=['isel0', 'isel1'], W=['idxf'], out=idxf[:],
                              in0=isel[:, 0, :], scalar=128.0, in1=isel[:, 1, :], op0=ALU.mult, op1=ALU.add)
                            A('dve', 'tensor_copy', R=['idxf'], W=['idx%d' % t], out=idx_all[:, t, :], in_=idxf[:])
                            A('dve', 'tensor_tensor', R=fvt, W=['gex'], out=gex[:], in0=fv[:],
                              in1=fv[:, :, 0:1].to_broadcast([128, 8, 16]), op=ALU.subtract)
                            A('act', 'activation', R=['gex'], W=['gex'], out=gex[:], in_=gex[:], func=AF.Exp)
                            A('dve', 'tensor_reduce', R=['gex'], W=['gsm'], out=gsm[:, 0, :], in_=gex[:], axis=AX.X,
                              op=ALU.add)
                            A('dve', 'reciprocal', R=['gsm'], W=['gsm'], out=gsm[:, 1, :], in_=gsm[:, 0, :])
                            A('dve', 'tensor_tensor', R=['gex', 'gsm'], W=['gate%d' % t],
                              out=gate_all[:, t, :].rearrange("p (h k) -> p h k", k=16), in0=gex[:],
                              in1=gsm[:, 1, :].unsqueeze(2).to_broadcast([128, 8, 16]), op=ALU.mult)
                        p1b_front(0)
                        for t in range(8):
                            p1b_mid(t)
                            if t + 1 < 8:
                                p1b_front(t + 1)
                            p1b_back(t)
                        if debug:
                            DMA('sp', 'dma_start', R=['idx%d' % t for t in range(8)], out=dbg['idx'][g, :, :],
                                in_=idx_all[:].rearrange("p t d -> p (t d)"))
                            DMA('sp', 'dma_start', R=['gate%d' % t for t in range(8)], out=dbg['gate'][g, :, :],
                                in_=gate_all[:].rearrange("p t d -> p (t d)"))
                        P.barrier()

                    with ExitStack() as p2s:
                        NV, ND, GJ = 22, 8, 4
                        vb32 = [sb("vb%d_%d" % (g, i), [128, D], F32, p2s) for i in range(NV)]
                        vb_ = [x[:].bitcast(BF16) for x in vb32]
                        dg_ = [sb("dg%d_%d" % (g, i), [128, 128], BF16, p2s) for i in range(ND)]
                        junk = [sb("junk%d_%d" % (g, i), [128, D], BF16, p2s) for i in range(4)]
                        apre = [sb("apre%d_%d" % (g, i), [128, 128], F32, p2s) for i in range(2)]
                        gact = [sb("gact%d_%d" % (g, i), [128, 128], F32, p2s) for i in range(2)]
                        htmp = sb("htmp%d" % g, [128, D], F32, p2s)
                        y2 = sb("y2_%d" % g, [128, D], F32, p2s)
                        outt = [sb("outt%d_%d" % (g, i), [128, D], F32, p2s) for i in range(2)]
                        stats = sb("stats2_%d" % g, [128, 2, 6], F32, p2s)
                        mv = sb("mv2_%d" % g, [128, 4], F32, p2s)
                        ph2 = [Q[0], Q[1]]
                        pvv = [Q[2], Q[3]]
                        import os as _os
                        ntiles = int(_os.environ.get('K_NT', '8')) if do_peer else 0
                        NJ = int(_os.environ.get('K_NJ', '128'))
                        if do_peer:
                            P.wait_sw('pool', ['conv%d' % i for i in range(32)])
                        for t in range(ntiles):
                            pb = t % 2
                            A('dve', 'tensor_tensor', R=['bc1'], W=['htmp'], out=htmp[:], in0=x1s[:, t, :], in1=bc[1][:],
                              op=ALU.mult)
                            A('dve', 'tensor_tensor', R=['htmp', 'bc2'], W=['ph2_%d' % pb], out=ph2[pb][:], in0=htmp[:],
                              in1=bc[2][:], op=ALU.add)
                            for j in range(NJ):
                                vs = j % NV
                                P.swdma('pool', 'indirect_dma_start', 'vb%d' % vs, W=['vb%d' % vs], out=vb32[vs][:],
                                        out_offset=None, in_=uv32[:, :],
                                        in_offset=bass.IndirectOffsetOnAxis(ap=idx_all[:, t, j:j + 1], axis=0))
                                A('dve', 'scalar_tensor_tensor', R=['vb%d' % vs, 'ph2_%d' % pb],
                                  W=['apre%d_%d' % (pb, j), 'junk%d' % (j % 4)], out=junk[j % 4][:], in0=vb_[vs][:, 0:D],
                                  scalar=1.0, in1=ph2[pb][:], op0=ALU.mult, op1=ALU.mult, accum_out=apre[pb][:, j:j + 1])
                                if j % GJ == GJ - 1:
                                    jg = j - GJ + 1
                                    gt = 'gact%d_%d' % (pb, j // GJ)
                                    A('act', 'activation', R=['apre%d_%d' % (pb, jj) for jj in range(jg, j + 1)],
                                      W=[gt], out=gact[pb][:, jg:jg + GJ], in_=apre[pb][:, jg:jg + GJ],
                                      func=AF.Gelu_apprx_tanh)
                                    A('dve', 'tensor_tensor', R=[gt], W=[gt], out=gact[pb][:, jg:jg + GJ],
                                      in0=gact[pb][:, jg:jg + GJ], in1=gate_all[:, t, jg:jg + GJ], op=ALU.mult)
                                    for jj in range(jg, j + 1):
                                        vs2, ds2 = jj % NV, jj % ND
                                        A('act', 'activation', R=[gt], W=['dg%d' % ds2], out=dg_[ds2][:], in_=ident_f[:],
                                          func=AF.Copy, scale=gact[pb][:, jj:jj + 1])
                                        for hf in range(2):
                                            A('pe', 'matmul', R=['dg%d' % ds2, 'vb%d' % vs2], W=['B%d%d' % (2 + pb, hf)],
                                              inc=(hf == 1), out=pvv[pb][:, hf * 512:(hf + 1) * 512], lhsT=dg_[ds2][:],
                                              rhs=vb_[vs2][:, D + hf * 512:D + (hf + 1) * 512], start=(jj == 0),
                                              stop=(jj == NJ - 1))
                            A('dve', 'tensor_tensor', R=['B%d0' % (2 + pb), 'B%d1' % (2 + pb), 'bc3'], W=['y2'], out=y2[:], in0=pvv[pb][:],
                              in1=bc[3][:], op=ALU.mult)
                            A('dve', 'scalar_tensor_tensor', R=['y2'], W=['y2'], out=y2[:], in0=x1s[:, t, :],
                              scalar=ALPHA, in1=y2[:], op0=ALU.mult, op1=ALU.add)
                            ob = t % 2
                            ln_block(A, y2, outt[ob][:], stats, mv, bvt, BV_L2G, BV_L2B, 'y2', 'outt%d' % ob)
                            DMA('sp', 'dma_start', R=['outt%d' % ob], out=yg[g][t * 128:(t + 1) * 128, :],
                                in_=outt[ob][:])
                        P.barrier()
        DMA('sp', 'dma_start', R=['rgout'], out=nrg[:, :], in_=rgout[:])
        P.finish()
        P.emit()


def make_in_maps(inp, ncores=NCORES):
    f = lambda a: np.ascontiguousarray(np.asarray(a, dtype=np.float32))
    x_prompt, x_sample = f(inp['x_prompt']), f(inp['x_sample'])
    c, c_ctx = f(inp['c']), f(inp['c_ctx'])
    st_hg, st_rg = f(inp['state_hgrn']), f(inp['state_rglru'])
    bv = np.concatenate([f(inp['hgrn_norm_g'])[0], f(inp['ln1_g'])[0], f(inp['ln1_b'])[0],
                         f(inp['ln2_g'])[0], f(inp['ln2_b'])[0]]).astype(np.float32)
    rgw = np.ascontiguousarray(np.stack([f(inp['rg_wr'])[0], f(inp['rg_wi'])[0]], axis=0))
    keys = np.ascontiguousarray(f(inp['peer_keys'])[0].reshape(16, 128, 128))
    shared = {
        'bv': bv, 'w_ada': f(inp['w_ada'])[0], 'w_in': f(inp['w_in'])[0], 'w_out': f(inp['w_out'])[0],
        'peer_wq': f(inp['peer_wq'])[0], 'peer_keys': keys, 'rgw': rgw,
        'peer_u': f(inp['peer_u'])[0], 'peer_v': f(inp['peer_v'])[0],
    }

    def pcol(v, n):
        return v.reshape(n, 128).T

    maps = []
    for cid in range(ncores):
        pp = np.zeros((128, NPP), np.float32)
        cond = np.stack([c_ctx, c[cid]], axis=0)
        pp[:, PP_COND:PP_COND + 16] = cond.reshape(2, 8, 128).transpose(2, 1, 0).reshape(128, 16)
        pp[:, PP_BADA:PP_BADA + 48] = pcol(f(inp['b_ada'])[0], 48)
        lb = f(inp['hgrn_lb'])
        pp[:, PP_LB:PP_LB + 16] = lb.reshape(2, 2, 4, 128).transpose(3, 0, 1, 2).reshape(128, 16)
        pp[:, PP_CW:PP_CW + 16] = f(inp['conv_w'])[0].reshape(4, 4, 128).transpose(2, 0, 1).reshape(128, 16)
        pp[:, PP_CB:PP_CB + 4] = pcol(f(inp['conv_b'])[0], 4)
        pp[:, PP_BR:PP_BR + 8] = f(inp['rg_br'])[0].reshape(2, 4, 128).transpose(2, 0, 1).reshape(128, 8)
        pp[:, PP_BI:PP_BI + 8] = f(inp['rg_bi'])[0].reshape(2, 4, 128).transpose(2, 0, 1).reshape(128, 8)
        pp[:, PP_LAM:PP_LAM + 8] = f(inp['rg_lam'])[0].reshape(2, 4, 128).transpose(2, 0, 1).reshape(128, 8)
        pp[:, PP_SRG:PP_SRG + 8] = st_rg[cid, 0].reshape(2, 4, 128).transpose(2, 0, 1).reshape(128, 8)
        m = dict(shared)
        m['xp'] = np.ascontiguousarray(x_prompt[4 * cid:4 * cid + 4].reshape(1024, D))
        m['xs'] = np.ascontiguousarray(x_sample[cid])
        m['pp'] = pp
        m['st_hg'] = np.ascontiguousarray(st_hg[cid, 0])
        maps.append(m)
    return maps


def assemble(rs):
    y_prompt = np.concatenate([r['yp'].reshape(4, 256, D) for r in rs], axis=0).astype(np.float32)
    y_sample = np.stack([r['ys'] for r in rs], axis=0).astype(np.float32)
    new_hg = np.concatenate([r['nhg'].reshape(4, 1, 2, 4, 128, 128) for r in rs], axis=0).astype(np.float32)
    new_rg = np.concatenate(
        [r['nrg'].reshape(128, 4, 2, 4).transpose(1, 2, 3, 0).reshape(4, 1, 2, 512) for r in rs], axis=0
    ).astype(np.float32)
    return (y_prompt, y_sample, new_hg, new_rg)


_NC_CACHE = {}


def kernel(**inputs):
    if 'nc' not in _NC_CACHE:
        _NC_CACHE['nc'] = build_program()
    nc = _NC_CACHE['nc']
    maps = make_in_maps(inputs)
    res = run_bass_kernel_spmd(nc, maps, core_ids=list(range(NCORES)))
    return assemble(res.results)
```
